# Optimizing a Trainium2 kernel written in Bass

```python
import math
import jax, jax.numpy as jnp
from jax import lax
import numpy as np

D_MODEL = 2048
BATCH = 8
SEQ = 2048
DEPTH = 2

CHUNK = 64
Q_BLOCK = 128
HEAD_DIM = 128
N_RET_HEADS = D_MODEL // (2 * HEAD_DIM)
N_FOX_HEADS = D_MODEL // (2 * HEAD_DIM)
N_DIFF_HEADS = D_MODEL // (2 * HEAD_DIM)
RET_W = N_RET_HEADS * HEAD_DIM
FOX_W = N_FOX_HEADS * HEAD_DIM
EVEN_IN = 4 * RET_W + 3 * FOX_W + N_FOX_HEADS
DIFF_QK = N_DIFF_HEADS * 2 * HEAD_DIM
DIFF_V = 2 * HEAD_DIM
ODD_IN = 2 * DIFF_QK + N_DIFF_HEADS * DIFF_V
D_FF = 4 * D_MODEL
ROPE_BASE = 10000.0
RMS_EPS = 1e-6
GN_EPS = 1e-5
FORGET_BIAS_INIT = 3.0
N_EVEN = (DEPTH + 1) // 2
N_ODD = DEPTH // 2

kernel_name = "chunk_causal_retention_fox_diffattn_hybrid"


def rms_norm(x, g):
    xf = x.astype(jnp.float32)
    y = xf * lax.rsqrt(jnp.mean(xf * xf, axis=-1, keepdims=True) + RMS_EPS)
    return (y * g.astype(jnp.float32)).astype(x.dtype)


def head_rms_norm(x, g):
    y = x * lax.rsqrt(jnp.mean(x * x, axis=-1, keepdims=True) + GN_EPS)
    return y * g.astype(jnp.float32)


def rotary(x, pos):
    half = x.shape[-1] // 2
    inv = ROPE_BASE ** (-jnp.arange(half, dtype=jnp.float32) / half)
    ang = pos.astype(jnp.float32)[:, None] * inv[None, :]
    cos = jnp.cos(ang)[None, :, None, :]
    sin = jnp.sin(ang)[None, :, None, :]
    xf = x.astype(jnp.float32)
    x1, x2 = xf[..., :half], xf[..., half:]
    return jnp.concatenate([x1 * cos - x2 * sin, x2 * cos + x1 * sin], axis=-1)


def retention_chunkwise(q, k, v):
    B, S, H, Dh = q.shape
    n = S // CHUNK
    log_g = jnp.log1p(-(2.0 ** (-5.0 - jnp.arange(H, dtype=jnp.float32))))
    q = q.astype(jnp.float32).reshape(B, n, CHUNK, H, Dh)
    k = (k.astype(jnp.float32) * (Dh ** -0.5)).reshape(B, n, CHUNK, H, Dh)
    v = v.astype(jnp.float32).reshape(B, n, CHUNK, H, Dh)
    idx = jnp.arange(CHUNK, dtype=jnp.float32)
    d_intra = jnp.exp(log_g[:, None, None] * jnp.abs(idx[:, None] - idx[None, :]))
    scores = jnp.einsum('bnihd,bnjhd->bnhij', q, k) * d_intra
    intra = jnp.einsum('bnhij,bnjhd->bnihd', scores, v)
    zeta = jnp.exp(log_g[None, :] * (CHUNK - 1.0 - idx)[:, None])
    kv = jnp.einsum('bnjhd,jh,bnjhe->nbhde', k, zeta, v)
    g_chunk = jnp.exp(log_g * CHUNK)[None, :, None, None]

    def step(state, kv_c):
        return state * g_chunk + kv_c, state

    _, prev = lax.scan(step, jnp.zeros((B, H, Dh, Dh), jnp.float32), kv)
    xi = jnp.exp(log_g[None, :] * (idx + 1.0)[:, None])
    cross = jnp.einsum('bnihd,nbhde,ih->bnihe', q, prev, xi)
    return (intra + cross).reshape(B, S, H, Dh)


def forgetting_attention(q, k, v, log_f):
    B, S, H, Dh = q.shape
    F = jnp.cumsum(log_f, axis=1).transpose(0, 2, 1)
    scale = Dh ** -0.5

    def block(b):
        q0, q1 = b * Q_BLOCK, (b + 1) * Q_BLOCK
        s = jnp.einsum('bqhd,bkhd->bhqk', q[:, q0:q1], k[:, :q1]).astype(jnp.float32) * scale
        bias = F[:, :, q0:q1, None] - F[:, :, None, :q1]
        mask = jnp.arange(q1)[None, :] <= jnp.arange(q0, q1)[:, None]
        p = jax.nn.softmax(jnp.where(mask, s + bias, -jnp.inf), axis=-1)
        return jnp.einsum('bhqk,bkhd->bqhd', p.astype(v.dtype), v[:, :q1])

    return jnp.concatenate([block(b) for b in range(S // Q_BLOCK)], axis=1)


def differential_attention(q, k, v, lam):
    B, S, H, _, Dh = q.shape
    scale = Dh ** -0.5

    def block(b):
        q0, q1 = b * Q_BLOCK, (b + 1) * Q_BLOCK
        s = jnp.einsum('bqhcd,bkhcd->bhcqk', q[:, q0:q1], k[:, :q1]).astype(jnp.float32) * scale
        mask = (jnp.arange(q1) // CHUNK)[None, :] <= (jnp.arange(q0, q1) // CHUNK)[:, None]
        p = jax.nn.softmax(jnp.where(mask, s, -jnp.inf), axis=-1)
        a = p[:, :, 0] - lam * p[:, :, 1]
        return jnp.einsum('bhqk,bkhe->bqhe', a, v[:, :q1].astype(jnp.float32))

    return jnp.concatenate([block(b) for b in range(S // Q_BLOCK)], axis=1)


def even_mixer(h, w_in, b_f, ret_gn, w_out, pos):
    B, S, _ = h.shape
    z = h @ w_in
    cuts = [RET_W, 2 * RET_W, 3 * RET_W, 4 * RET_W,
            4 * RET_W + FOX_W, 4 * RET_W + 2 * FOX_W, 4 * RET_W + 3 * FOX_W]
    rq, rk, rv, rg, fq, fk, fv, ff = jnp.split(z, cuts, axis=-1)
    rh = lambda t: t.reshape(B, S, N_RET_HEADS, HEAD_DIM)
    fh = lambda t: t.reshape(B, S, N_FOX_HEADS, HEAD_DIM)
    ret = retention_chunkwise(rotary(rh(rq), pos), rotary(rh(rk), pos), rh(rv))
    ret = head_rms_norm(ret, ret_gn.reshape(N_RET_HEADS, HEAD_DIM)).reshape(B, S, RET_W)
    ret = (jax.nn.silu(rg.astype(jnp.float32)) * ret).astype(h.dtype)
    log_f = jax.nn.log_sigmoid((ff + b_f).astype(jnp.float32))
    fox = forgetting_attention(fh(fq), fh(fk), fh(fv), log_f).reshape(B, S, FOX_W)
    return jnp.concatenate([ret, fox.astype(h.dtype)], axis=-1) @ w_out


def odd_mixer(h, w_in, lq1, lk1, lq2, lk2, subln_g, w_out, lambda_init):
    B, S, _ = h.shape
    z = h @ w_in
    q = z[..., :DIFF_QK].reshape(B, S, N_DIFF_HEADS, 2, HEAD_DIM)
    k = z[..., DIFF_QK:2 * DIFF_QK].reshape(B, S, N_DIFF_HEADS, 2, HEAD_DIM)
    v = z[..., 2 * DIFF_QK:].reshape(B, S, N_DIFF_HEADS, DIFF_V)
    f32 = jnp.float32
    lam = (jnp.exp(jnp.sum(lq1.astype(f32) * lk1.astype(f32)))
           - jnp.exp(jnp.sum(lq2.astype(f32) * lk2.astype(f32))) + lambda_init)
    o = differential_attention(q, k, v, lam)
    o = head_rms_norm(o, subln_g[None, :]) * (1.0 - lambda_init)
    return o.reshape(B, S, N_DIFF_HEADS * DIFF_V).astype(h.dtype) @ w_out


def squared_relu_mlp(h, w1, w2):
    a = jax.nn.relu(h @ w1)
    return (a * a) @ w2


def setup_inputs(seed: int = 0) -> dict:
    key = jax.random.key(seed)
    ks = jax.random.split(key, 20)
    nrm = lambda k, shape, s: jax.random.normal(k, shape, jnp.float32) * s
    return {
        "x": nrm(ks[0], (BATCH, SEQ, D_MODEL), 1.0),
        "norm_mix_g": 1.0 + nrm(ks[1], (DEPTH, D_MODEL), 0.02),
        "norm_mlp_g": 1.0 + nrm(ks[2], (DEPTH, D_MODEL), 0.02),
        "even_w_in": nrm(ks[3], (N_EVEN, D_MODEL, EVEN_IN), D_MODEL ** -0.5),
        "even_b_f": FORGET_BIAS_INIT + nrm(ks[4], (N_EVEN, N_FOX_HEADS), 0.1),
        "even_ret_gn": 1.0 + nrm(ks[5], (N_EVEN, RET_W), 0.02),
        "even_w_out": nrm(ks[6], (N_EVEN, RET_W + FOX_W, D_MODEL), (RET_W + FOX_W) ** -0.5),
        "odd_w_in": nrm(ks[7], (N_ODD, D_MODEL, ODD_IN), D_MODEL ** -0.5),
        "odd_lambda_q1": nrm(ks[8], (N_ODD, HEAD_DIM), 0.1),
        "odd_lambda_k1": nrm(ks[9], (N_ODD, HEAD_DIM), 0.1),
        "odd_lambda_q2": nrm(ks[10], (N_ODD, HEAD_DIM), 0.1),
        "odd_lambda_k2": nrm(ks[11], (N_ODD, HEAD_DIM), 0.1),
        "odd_subln_g": 1.0 + nrm(ks[12], (N_ODD, DIFF_V), 0.02),
        "odd_w_out": nrm(ks[13], (N_ODD, N_DIFF_HEADS * DIFF_V, D_MODEL), (N_DIFF_HEADS * DIFF_V) ** -0.5),
        "mlp_w1": nrm(ks[14], (DEPTH, D_MODEL, D_FF), D_MODEL ** -0.5),
        "mlp_w2": nrm(ks[15], (DEPTH, D_FF, D_MODEL), D_FF ** -0.5),
        "final_g": 1.0 + nrm(ks[16], (D_MODEL,), 0.02),
    }


def reference(x, norm_mix_g, norm_mlp_g, even_w_in, even_b_f, even_ret_gn, even_w_out,
              odd_w_in, odd_lambda_q1, odd_lambda_k1, odd_lambda_q2, odd_lambda_k2,
              odd_subln_g, odd_w_out, mlp_w1, mlp_w2, final_g):
    pos = jnp.arange(x.shape[1], dtype=jnp.int32)
    h = x
    for i in range(DEPTH):
        j = i // 2
        hn = rms_norm(h, norm_mix_g[i])
        if i % 2 == 0:
            mix = even_mixer(hn, even_w_in[j], even_b_f[j], even_ret_gn[j], even_w_out[j], pos)
        else:
            lambda_init = 0.8 - 0.6 * math.exp(-0.3 * i)
            mix = odd_mixer(hn, odd_w_in[j], odd_lambda_q1[j], odd_lambda_k1[j],
                            odd_lambda_q2[j], odd_lambda_k2[j], odd_subln_g[j],
                            odd_w_out[j], lambda_init)
        h = h + mix.astype(h.dtype)
        h = h + squared_relu_mlp(rms_norm(h, norm_mlp_g[i]), mlp_w1[i], mlp_w2[i]).astype(h.dtype)
    return rms_norm(h, final_g)
```

```python
import contextlib
import math
import numpy as np
import concourse.bass as bass
import concourse.mybir as mybir
from concourse.bass_utils import run_bass_kernel_spmd

F32 = mybir.dt.float32
BF16 = mybir.dt.bfloat16
AF = mybir.ActivationFunctionType
ALU = mybir.AluOpType
AX = mybir.AxisListType

D = 2048
T = 2048
KC = 16
TB = 16
HD = 128
DFF = 8192
EVEN_IN = 7176
ODD_IN = 6144
RMS_EPS = 1e-6
GN_EPS = 1e-5
LAMBDA_INIT = 0.8 - 0.6 * math.exp(-0.3 * 1)
SCALE = HD ** -0.5

ENGS = ["sync", "scalar", "gpsimd", "vector", "tensor"]


class Prog:
    def __init__(self, nc):
        self.nc = nc
        self.q = {e: [] for e in ENGS}
        self.cnt = {e: 0 for e in ENGS}
        self.dsem = {}
        self.seen = {e: {} for e in ENGS}
        self.dtoks = {}

    def _flat(self, waits, out):
        for w in waits:
            if w is None:
                continue
            if isinstance(w, list):
                self._flat(w, out)
            else:
                out.append(w)
        return out

    def _ws(self, eng, waits):
        ws = []
        for key, val in self._flat(waits, []):
            if self.seen[eng].get(key, 0) >= val:
                continue
            self.seen[eng][key] = val
            ws.append((key, val))
        return ws

    def op(self, eng, fn, waits=(), sig=True):
        ws = self._ws(eng, waits)
        tok = None
        if sig:
            self.cnt[eng] += 1
            tok = (("p", eng), self.cnt[eng])
        self.q[eng].append((fn, ws, ("p", eng) if sig else None, 1))
        return tok

    def dma(self, eng, out, in_, sem, waits=()):
        ws = self._ws(eng, waits)
        self.dsem.setdefault(sem, 0)
        self.dsem[sem] += 16
        tok = (("d", sem), self.dsem[sem])
        self.q[eng].append((lambda e: e.dma_start(out=out, in_=in_), ws, ("d", sem), 16))
        self.dtoks[sem] = tok
        return tok

    def mm(self, out, lhsT, rhs, start, stop, waits=(), sig=False):
        return self.op("tensor", lambda e: e.matmul(out, lhsT=lhsT, rhs=rhs, start=start, stop=stop), waits, sig)

    def tr(self, out, in_, ident, waits=(), sig=False):
        return self.op("tensor", lambda e: e.transpose(out=out, in_=in_, identity=ident), waits, sig)

    def act(self, out, in_, func, waits=(), sig=True, **kw):
        return self.op("scalar", lambda e: e.activation(out=out, in_=in_, func=func, **kw), waits, sig)

    def tt(self, eng, out, in0, in1, op, waits=(), sig=True):
        return self.op(eng, lambda e: e.tensor_tensor(out=out, in0=in0, in1=in1, op=op), waits, sig)

    def ts(self, eng, out, in0, s1, s2, op0, op1=None, waits=(), sig=True):
        if op1 is None:
            return self.op(eng, lambda e: e.tensor_scalar(out=out, in0=in0, scalar1=s1, scalar2=None, op0=op0), waits, sig)
        return self.op(eng, lambda e: e.tensor_scalar(out=out, in0=in0, scalar1=s1, scalar2=s2, op0=op0, op1=op1), waits, sig)

    def stt(self, out, in0, scalar, in1, op0, op1, waits=(), sig=True, accum_out=None):
        if accum_out is None:
            return self.op("vector", lambda e: e.scalar_tensor_tensor(out=out, in0=in0, scalar=scalar, in1=in1, op0=op0, op1=op1), waits, sig)
        return self.op("vector", lambda e: e.scalar_tensor_tensor(out=out, in0=in0, scalar=scalar, in1=in1, op0=op0, op1=op1, accum_out=accum_out), waits, sig)

    def recip(self, out, in_, waits=(), sig=True):
        return self.op("vector", lambda e: e.reciprocal(out=out, in_=in_), waits, sig)

    def memset(self, eng, ap, val, waits=(), sig=True):
        return self.op(eng, lambda e: e.memset(ap, val), waits, sig)

    def cp(self, eng, out, in_, waits=(), sig=True):
        if eng == "scalar":
            return self.op(eng, lambda e: e.copy(out=out, in_=in_), waits, sig)
        return self.op(eng, lambda e: e.tensor_copy(out=out, in_=in_), waits, sig)

    def barrier(self, scr, skip=None):
        toks = []
        toks.append(self.op("vector", lambda e: e.memset(scr[0:1, 0:1], 0.0)))
        toks.append(self.op("gpsimd", lambda e: e.memset(scr[0:1, 1:2], 0.0)))
        toks.append(self.op("scalar", lambda e: e.copy(out=scr[0:1, 3:4], in_=scr[0:1, 2:3])))
        if self.cnt["tensor"] > 0:
            toks.append((("p", "tensor"), self.cnt["tensor"]))
        toks += [t for s, t in self.dtoks.items() if not (skip and s.startswith(skip))]
        self.bar_toks = toks
        return toks

    def emit(self, final_waits):
        nc = self.nc
        with contextlib.ExitStack() as st:
            sems = {}
            for e in ENGS:
                sems[("p", e)] = st.enter_context(nc.semaphore("p_" + e))
            for s in self.dsem:
                sems[("d", s)] = st.enter_context(nc.semaphore("d_" + s))
            block = st.enter_context(nc.Block())

            def runner(ename):
                def _(eng):
                    for fn, ws, sg, inc in self.q[ename]:
                        for key, val in ws:
                            eng.wait_ge(sems[key], val)
                        ins = fn(eng)
                        if sg is not None:
                            ins.then_inc(sems[sg], inc)
                    if ename == "sync":
                        for key, val in final_waits:
                            eng.wait_ge(sems[key], val)
                return _

            block.sync(runner("sync"))
            block.scalar(runner("scalar"))
            block.gpsimd(runner("gpsimd"))
            block.vector(runner("vector"))
            block.tensor(runner("tensor"))


class Arena:
    def __init__(self, tensor, nbytes):
        self.t = tensor
        self.nbytes = nbytes
        self.off = 0
        self.base = 0

    def reset(self, base=None):
        self.off = self.base if base is None else base

    def alloc(self, shape, dt):
        esz = 4 if dt == F32 else 2
        n = int(np.prod(shape))
        nb = (n * esz + 31) // 32 * 32
        assert self.off + nb <= self.nbytes, ("SBUF arena overflow", self.off, nb, self.nbytes)
        w0 = self.off // 4
        w1 = (self.off + nb) // 4
        self.off += nb
        ap = self.t[:, w0:w1]
        if dt != F32:
            ap = ap.bitcast(dt)
        ap = ap[:, 0:n]
        if len(shape) == 2:
            ap = ap.rearrange("p (a b) -> p a b", b=shape[1])
        elif len(shape) == 3:
            ap = ap.rearrange("p (a b c) -> p a b c", b=shape[1], c=shape[2])
        elif len(shape) == 4:
            ap = ap.rearrange("p (a b c d) -> p a b c d", b=shape[1], c=shape[2], d=shape[3])
        return ap


class Ring:
    def __init__(self, bufs):
        self.bufs = bufs
        self.n = len(bufs)
        self.i = 0
        self.free = [[] for _ in bufs]

    def next(self):
        k = self.i % self.n
        self.i += 1
        return k, self.bufs[k], self.free[k]

    def release(self, k, toks):
        self.free[k] = [t for t in toks if t is not None]


ARENA_BYTES = 200 * 1024


def build_program(stop_after=None, nlayers=2, debug=False):
    nc = bass.Bass("TRN2", target_bir_lowering=False)

    def din(name, shape, dt=F32):
        return nc.dram_tensor(name, list(shape), dt, kind="ExternalInput").ap()

    x = din("x", [T, D])
    norm_mix_g = din("norm_mix_g", [2, D])
    norm_mlp_g = din("norm_mlp_g", [2, D])
    even_w_in = din("even_w_in", [D, EVEN_IN])
    even_b_f = din("even_b_f", [8, 1])
    even_ret_gn = din("even_ret_gn", [128, 8])
    even_w_out = din("even_w_out", [D, D])
    odd_w_in = din("odd_w_in", [D, ODD_IN])
    odd_lam = din("odd_lam", [4, 128])
    odd_subln_g = din("odd_subln_g", [1, 256])
    odd_w_out = din("odd_w_out", [D, D])
    mlp_w1 = din("mlp_w1", [2, D, DFF])
    mlp_w2 = din("mlp_w2", [2, DFF, D])
    final_g = din("final_g", [1, D])
    c_rope = din("c_rope", [128, 2, T])
    c_rot = din("c_rot", [128, 128])
    c_dec = din("c_dec", [8, 128, T])
    c_mask = din("c_mask", [128, 2, 128])
    c_sel = din("c_sel", [8, 8, 128])
    c_id8 = din("c_id8", [8, 8])
    y = nc.dram_tensor("y", [T, D], F32, kind="ExternalOutput").ap()
    ikind = "ExternalOutput" if debug else "Internal"
    hA = nc.dram_tensor("hA", [T, D], F32, kind=ikind).ap()
    hB = nc.dram_tensor("hB", [T, D], F32, kind=ikind).ap()
    oTd = nc.dram_tensor("oTd", [D, T], BF16, kind=ikind).ap()

    with contextlib.ExitStack() as st:
        arena_t = st.enter_context(nc.sbuf_tensor("arena", [128, ARENA_BYTES // 4], F32))
        psF = [st.enter_context(nc.psum_tensor("psF%d" % i, [128, 512], F32))[:, :] for i in range(6)]
        psT = [st.enter_context(nc.psum_tensor("psT%d" % i, [128, 1024], BF16))[:, :] for i in range(2)]
        P = Prog(nc)
        A = Arena(arena_t, ARENA_BYTES)

        ident = A.alloc([128], BF16)
        scr = A.alloc([8], F32)
        neghalf = A.alloc([1], F32)
        t0 = P.op("gpsimd", lambda e: e.memset(ident, 0.0))
        t_id = P.op("gpsimd", lambda e: e.affine_select(out=ident, in_=ident, pattern=[[-1, 128]],
                                                        compare_op=ALU.not_equal, fill=1.0, base=0,
                                                        channel_multiplier=1), waits=[t0])
        t1 = P.op("gpsimd", lambda e: e.memset(scr, 0.0))
        t_nh = P.op("gpsimd", lambda e: e.memset(neghalf, -0.5))
        A.base = A.off
        init_toks = [t_id, t1, t_nh]

        bankF_free = [[] for _ in range(6)]
        bankT_free = [[] for _ in range(2)]

        def barrier():
            b = P.barrier(scr)
            for l in bankF_free:
                del l[:]
            for l in bankT_free:
                del l[:]
            return b

        def rstd_of(ss_col, v_col, r_col, n, eps, waits):
            t_v = P.ts("vector", v_col, ss_col, 1.0 / n, eps, ALU.mult, ALU.add, waits=waits)
            return P.tt("gpsimd", r_col, v_col, neghalf[:, 0:1], ALU.pow, waits=[t_v, t_nh])

        def norm_s1a(src, t_src, bi, tmp):
            t_ss = P.act(tmp["junk"], src, AF.Square, waits=t_src + init_toks, accum_out=tmp["ss"][:, bi:bi + 1])
            t_r = rstd_of(tmp["ss"][:, bi:bi + 1], tmp["v"][:, bi:bi + 1], tmp["rs"][:, bi:bi + 1], D, RMS_EPS, [t_ss])
            return t_ss, t_r

        def norm_s1b(src, t_src, t_r, g_bc, t_g, bi, tmp, xs_free):
            s = bi % 2
            return P.stt(tmp["xs"][:, s, :], src, tmp["rs"][:, bi:bi + 1], g_bc, ALU.mult, ALU.mult,
                         waits=[t_r, t_g] + t_src + xs_free[s])

        def norm_pipeline(nblk, get_src, g_bc, t_g, dstT, tmp, xs_free, on_iter=None, after_s2=None):
            st = {}
            readers = {}
            for k in range(nblk + 2):
                if on_iter is not None:
                    on_iter(k, readers)
                if k < nblk:
                    src, t_src = get_src(k)
                    t_ss, t_r = norm_s1a(src, t_src, k, tmp)
                    st[k] = [src, t_src, t_ss, t_r, None]
                if 0 <= k - 1 < nblk:
                    src, t_src, t_ss, t_r, _ = st[k - 1]
                    t_x = norm_s1b(src, t_src, t_r, g_bc, t_g, k - 1, tmp, xs_free)
                    st[k - 1][4] = t_x
                    readers[k - 1] = [t_ss, t_x]
                if 0 <= k - 2 < nblk:
                    evs = norm_s2(st[k - 2][4], dstT, (k - 2) * 128, k - 2, tmp, xs_free)
                    if after_s2 is not None:
                        after_s2(k - 2, evs)
            return readers

        def norm_s2(t_x, dstT, col, bi, tmp, xs_free):
            s = bi % 2
            evs = []
            for kg in range(2):
                bank = psT[kg]
                for j in range(8):
                    kc = kg * 8 + j
                    tk = P.tr(bank[:, j * 128:(j + 1) * 128], tmp["xs"][:, s, kc * 128:(kc + 1) * 128], ident,
                              waits=[t_x, t_id] + bankT_free[kg], sig=(j == 7))
                eng = "scalar" if kg == 0 else "vector"
                t_e = P.cp(eng, dstT[:, kg * 8:(kg + 1) * 8, col:col + 128],
                           bank.rearrange("p (j c) -> p j c", j=8), waits=[tk])
                bankT_free[kg] = [t_e]
                evs.append(t_e)
            xs_free[s] = [tk]
            return evs

        def alloc_norm_tmp():
            return {"junk": A.alloc([D], BF16), "xs": A.alloc([2, D], BF16), "ss": A.alloc([TB], F32),
                    "v": A.alloc([TB], F32), "rs": A.alloc([TB], F32)}

        def proj_fm(wslab, c0, xnT, waits, evac):
            for tq in range(4):
                b = tq % 2
                for kc in range(KC):
                    tk = P.mm(psF[b][:, :], wslab[:, kc, c0:c0 + 128], xnT[:, kc, tq * 512:(tq + 1) * 512],
                              kc == 0, kc == KC - 1, waits=waits + bankF_free[b], sig=(kc == KC - 1))
                bankF_free[b] = [evac(psF[b], tq, [tk])]
            return tk

        def proj_tm(wslab, ncols, xnT, waits, evac):
            for tb in range(TB):
                b = tb % 2
                for kc in range(KC):
                    tk = P.mm(psF[b][:, 0:ncols], xnT[:, kc, tb * 128:(tb + 1) * 128], wslab[:, kc, 0:ncols],
                              kc == 0, kc == KC - 1, waits=waits + bankF_free[b], sig=(kc == KC - 1))
                bankF_free[b] = [evac(psF[b], tb, [tk])]
            return tk

        def attention(nmaps, QTs, KTs, vaug, dvp, t_in, elementwise, epilogue, pT, pT_free, filler=None, sbanks=None):
            items = []
            last_ew = [None]
            for i in range(TB):
                j = i
                while j >= 0:
                    n = min(4, j + 1)
                    items.append((i, j, n, j - n < 0))
                    j -= n
            sidx = 0
            lasts = {}
            deferred = [None]
            if sbanks is None:
                sbanks = [0, 1, 3, 5] if nmaps == 1 else [0, 1]
            depth = 3 if nmaps == 1 else 1
            queue = []

            ep1, ep2 = epilogue

            for k in range(len(items) + 1):
                if k < len(items):
                    i, jtop, n, is_last = items[k]
                    its = []
                    for m in range(nmaps):
                        sb = sbanks[sidx % len(sbanks)]
                        pk = sidx % len(pT)
                        sidx += 1
                        for c in range(n):
                            jj = jtop - c
                            tk = P.mm(psF[sb][:, c * 128:(c + 1) * 128], KTs[m][:, jj * 128:(jj + 1) * 128],
                                      QTs[m][:, i * 128:(i + 1) * 128], True, True,
                                      waits=t_in + bankF_free[sb], sig=(c == n - 1))
                        t_e = elementwise(m, i, jtop, n, psF[sb], pT[pk], [tk] + pT_free[pk])
                        bankF_free[sb] = [t_e]
                        last_ew[0] = t_e
                        its.append((m, pk, t_e))
                    queue.append((i, jtop, n, is_last, its))
                while queue and (len(queue) > depth or k >= len(items)):
                    pi, pjtop, pn, plast, pitems = queue.pop(0)
                    for (m, pk, t_e) in pitems:
                        bidx = 2 + 2 * (pi % 2) + m
                        for c in range(pn):
                            jj = pjtop - c
                            tk = P.mm(psF[bidx][:, 0:dvp], pT[pk][:, c * 128:(c + 1) * 128], vaug(jj),
                                      jj == pi, jj == 0, waits=[t_e] + t_in + bankF_free[bidx], sig=(c == pn - 1))
                        lasts[(pi, m)] = tk
                        pT_free[pk] = [tk]
                    if filler is not None:
                        next(filler, None)
                    if plast:
                        frees, state = ep1(pi, [lasts[(pi, m)] for m in range(nmaps)])
                        for m in range(nmaps):
                            bankF_free[2 + 2 * (pi % 2) + m] = frees
                        if deferred[0] is not None:
                            ep2(*deferred[0])
                        deferred[0] = (pi, state)
            if deferred[0] is not None:
                ep2(*deferred[0])
            return last_ew[0]

        class WRing:
            def __init__(self, n, shape, name):
                self.bufs = [A.alloc(shape, BF16) for _ in range(n)]
                self.free = [[] for _ in range(n)]
                self.i = 0
                self.name = name
                self.n = n

            def load(self, src_ap, extra_waits=(), sub=None):
                k = self.i % self.n
                self.i += 1
                dst = self.bufs[k] if sub is None else sub(self.bufs[k])
                t = P.dma("gpsimd", dst, src_ap, "%s%d" % (self.name, k), waits=list(extra_waits) + self.free[k])
                return k, self.bufs[k], t

            def release(self, k, toks):
                self.free[k] = [t for t in toks if t is not None]

        final_tok = []
        done = False
        for layer in range(nlayers):
            src_h = x if layer == 0 else hB
            mid_h = hA
            dst_h = hB
            w_in = even_w_in if layer == 0 else odd_w_in
            w_out = even_w_out if layer == 0 else odd_w_out
            w_in_r = w_in.rearrange("(kc p) n -> p kc n", p=128)

            bar = barrier()
            A.reset()
            xnT = A.alloc([KC, T], BF16)
            wr = WRing(4, [KC, 128 if layer == 0 else 256], "wr")
            wrv0 = WRing(1, [KC, 512], "wrv")
            V0 = A.alloc([TB, 4, 128] if layer == 0 else [TB, 2, 258], BF16)
            baseB = A.off
            if layer == 0:
                pre_w = [wr.load(w_in_r[:, :, c0:c0 + 128], bar) for c0 in (0, 1024, 3072)]
                pre_v = wrv0.load(w_in_r[:, :, 2048:2048 + 512], bar)
            else:
                pre_w = [wr.load(w_in_r[:, :, c0:c0 + 256], bar) for c0 in (0, 2048)]
                pre_v = wrv0.load(w_in_r[:, :, 4096:4096 + 512], bar)
            g_bc = A.alloc([D], F32)
            hb = A.alloc([4, D], F32)
            tmpn = alloc_norm_tmp()
            t_g = P.dma("sync", g_bc, norm_mix_g[layer:layer + 1, :].partition_broadcast(128), "gbc", waits=bar)
            xs_free = [[], []]
            t_ls = {}

            def a_load(bq, readers):
                if 0 <= bq < TB and bq not in t_ls:
                    sl_ = bq % 4
                    t_ls[bq] = P.dma("sync", hb[:, sl_, :], src_h[bq * 128:(bq + 1) * 128, :], "hb%d" % sl_,
                                     waits=bar + (readers.get(bq - 4) or []))
            a_load(0, {})
            a_load(1, {})

            def a_iter(k, readers):
                a_load(k + 2, readers)
            vst = {"evs": {}, "toks": [], "tk": None}

            def v_block(tb):
                b = tb % 2
                k_v, w_v, t_wv = pre_v
                for kc in range(KC):
                    tk = P.mm(psF[b][:, 0:512], xnT[:, kc, tb * 128:(tb + 1) * 128], w_v[:, kc, 0:512],
                              kc == 0, kc == KC - 1, waits=[t_wv] + bar + vst["evs"][tb] + bankF_free[b], sig=(kc == KC - 1))
                if layer == 0:
                    t_e = P.cp("vector", V0[:, tb, :, :], psF[b][:, 0:512].rearrange("p (g d) -> p g d", g=4), waits=[tk])
                else:
                    t_e = P.cp("vector", V0[:, tb, :, 0:256], psF[b][:, 0:512].rearrange("p (g d) -> p g d", g=2), waits=[tk])
                bankF_free[b] = [t_e]
                vst["toks"] = (vst["toks"] + [t_e])[-2:]
                vst["tk"] = tk

            def a_after(tb, evs):
                vst["evs"][tb] = evs
                if tb >= 1:
                    v_block(tb - 1)
            norm_pipeline(TB, lambda bq: (hb[:, bq % 4, :], [t_ls[bq]] + bar), g_bc, t_g, xnT, tmpn, xs_free,
                          on_iter=a_iter, after_s2=a_after)
            v_block(TB - 1)
            wrv0.release(pre_v[0], [vst["tk"]])
            t_v_pre = list(vst["toks"])
            if layer == 0:
                pre_vnext = wrv0.load(w_in_r[:, :, 2048 + 512:2048 + 1024], bar)
            else:
                pre_vnext = wrv0.load(w_in_r[:, :, 4096 + 512:4096 + 1024], bar)
            if stop_after == ("A", layer):
                bar = barrier()
                final_tok = [P.dma("sync", oTd.rearrange("(kc p) t -> p kc t", p=128), xnT, "dbg", waits=bar)]
                done = True
                break

            bar = barrier()
            A.reset(baseB)
            pT = [A.alloc([512], BF16) for _ in range(4)]
            pT_free = [[] for _ in pT]
            cm = A.alloc([2, 128], BF16)
            t_cm = P.dma("gpsimd", cm, c_mask, "cm", waits=bar)
            sm = A.alloc([3, 16], F32)
            baseB2 = A.off

            if layer == 0:
                cs = A.alloc([2, T], F32)
                rot = A.alloc([128], BF16)
                gn = A.alloc([8], F32)
                t_cs = P.dma("sync", cs, c_rope, "cs", waits=bar)
                t_rot = P.dma("gpsimd", rot, c_rot, "rot", waits=bar)
                t_gn = P.dma("sync", gn, even_ret_gn, "gn", waits=bar)
                raw = A.alloc([T], BF16)
                QT2 = [A.alloc([T], BF16) for _ in range(2)]
                KT2 = [A.alloc([T], BF16) for _ in range(2)]
                V = V0
                wrv = wrv0
                siluT2 = [A.alloc([T], BF16) for _ in range(2)]
                dec2 = [A.alloc([T], F32) for _ in range(2)]
                Abuf = A.alloc([T], BF16)
                rt = [A.alloc([512], F32) for _ in range(2)]
                otm = [A.alloc([128], F32) for _ in range(2)]
                obf = [A.alloc([128], BF16) for _ in range(2)]
                sq = A.alloc([128], F32)
                oTs2 = [A.alloc([T], BF16) for _ in range(2)]
                cols = [0, 1024, 3072]
                otm_free = [[], []]
                obf_free = [[], []]
                hist_ew = {}
                hist_ep = {}
                hist_st = {}
                pw = {}
                pst = {}
                raw_free = [[]]
                abuf_free = [[]]
                rt_free = [[], []]
                ricnt = [0]

                def load_w(h):
                    if h == 0:
                        pw[0] = pre_w
                        return
                    pw[h] = [wr.load(w_in_r[:, :, c0 + h * 128:c0 + (h + 1) * 128], bar) for c0 in cols]

                vpre = {}

                def v_proj(g):
                    k_v, w_v, t_wv = pre_v if g == 0 else vpre[g]
                    tk = proj_tm(w_v, 512, xnT, [t_wv] + bar,
                                 lambda bank, tb, waits: P.cp("vector", V[:, tb, :, :],
                                                              bank[:, 0:512].rearrange("p (g d) -> p g d", g=4), waits=waits))
                    wrv.release(k_v, [tk])
                    if g + 1 < 2:
                        vpre[g + 1] = wrv.load(w_in_r[:, :, 2048 + (g + 1) * 512:2048 + (g + 2) * 512], bar)
                    return bankF_free[0] + bankF_free[1]

                def P_gen(h):
                    siluT, QT, KT = siluT2[h % 2], QT2[h % 2], KT2[h % 2]
                    (k_q, w_q, t_wq), (k_k, w_k, t_wk), (k_g, w_g, t_wg) = pw[h]
                    t_silu = []
                    for tq in range(4):
                        b = tq % 2
                        for kc in range(KC):
                            tk = P.mm(psF[b][:, :], w_g[:, kc, 0:128], xnT[:, kc, tq * 512:(tq + 1) * 512],
                                      kc == 0, kc == KC - 1, waits=[t_wg] + bar + bankF_free[b], sig=(kc == KC - 1))
                        t_e = P.act(siluT[:, tq * 512:(tq + 1) * 512], psF[b], AF.Silu, waits=[tk, hist_ep.get(h - 2)])
                        bankF_free[b] = [t_e]
                        t_silu.append(t_e)
                        yield
                    wr.release(k_g, [tk])
                    t_qk = []
                    for qi, (wsl, t_w, kk, dst) in enumerate(((w_q, t_wq, k_q, QT), (w_k, t_wk, k_k, KT))):
                        a_toks = []
                        ev_toks = []
                        for tq in range(4):
                            b = tq % 2
                            sl = slice(tq * 512, (tq + 1) * 512)
                            for kc in range(KC):
                                tk = P.mm(psF[b][:, :], wsl[:, kc, 0:128], xnT[:, kc, sl],
                                          kc == 0, kc == KC - 1, waits=[t_w] + bar + bankF_free[b], sig=(kc == KC - 1))
                            t_e = P.act(raw[:, sl], psF[b], AF.Copy, waits=[tk] + raw_free[0])
                            bankF_free[b] = [t_e]
                            ev_toks.append(t_e)
                            a_toks.append(P.tt("gpsimd", Abuf[:, sl], raw[:, sl], cs[:, 0, sl], ALU.mult,
                                               waits=[t_e, t_cs] + abuf_free[0]))
                            yield
                        wr.release(kk, [tk])
                        tcs = []
                        for tq in range(4):
                            b = tq % 2
                            sl = slice(tq * 512, (tq + 1) * 512)
                            tk = P.mm(psF[b][:, :], rot, raw[:, sl], True, True,
                                      waits=ev_toks + [t_rot] + bankF_free[b], sig=True)
                            r = ricnt[0] % 2
                            tb_ = P.tt("vector", rt[r], psF[b][:, :], cs[:, 1, sl], ALU.mult, waits=[tk, t_cs] + rt_free[r])
                            bankF_free[b] = [tb_]
                            eng = "gpsimd" if ricnt[0] % 2 == 0 else "vector"
                            tc = P.tt(eng, dst[:, sl], rt[r], Abuf[:, sl], ALU.add, waits=[tb_] + a_toks)
                            rt_free[r] = [tc]
                            tcs.append(tc)
                            ricnt[0] += 1
                            yield
                        raw_free[0] = [tk, a_toks[-1]]
                        abuf_free[0] = tcs[-2:]
                        t_qk += tcs
                    if h + 1 < 8:
                        load_w(h + 1)
                    pst[h] = {"t_in": t_qk, "t_silu": t_silu}

                load_w(0)
                t_v = t_v_pre
                vpre[1] = pre_vnext
                for _ in P_gen(0):
                    pass
                t_decs = {0: P.dma("sync", dec2[0], c_dec[0], "dec0", waits=bar)}
                for h in range(8):
                    siluT, QT, KT = siluT2[h % 2], QT2[h % 2], KT2[h % 2]
                    dec = dec2[h % 2]
                    oTs = oTs2[h % 2]
                    t_dec = t_decs[h]
                    if h + 1 < 8:
                        t_decs[h + 1] = P.dma("sync", dec2[(h + 1) % 2], c_dec[h + 1], "dec%d" % ((h + 1) % 2),
                                              waits=bar + [hist_ew.get(h - 1)])
                    t_in = pst[h]["t_in"] + t_v
                    t_silu = pst[h]["t_silu"]

                    def elementwise(m, i, jtop, n, ps, pt, waits, dec=dec, t_dec=t_dec):
                        o = (i - jtop) * 128
                        return P.tt("vector", pt[:, 0:n * 128], ps[:, 0:n * 128], dec[:, o:o + n * 128], ALU.mult,
                                    waits=waits + [t_dec])

                    ep_last = [None]

                    def ep1(i, lasts):
                        s = i % 2
                        acc = psF[2 + 2 * (i % 2)][:, 0:128]
                        t_sq = P.act(sq, acc, AF.Square, waits=lasts, accum_out=sm[:, 0, i:i + 1])
                        t_c = P.act(otm[s], acc, AF.Copy, waits=lasts + otm_free[s])
                        t_v_ = P.ts("gpsimd", sm[:, 1, i:i + 1], sm[:, 0, i:i + 1], 1.0 / 128, GN_EPS, ALU.mult, ALU.add, waits=[t_sq])
                        t_r = P.tt("gpsimd", sm[:, 2, i:i + 1], sm[:, 1, i:i + 1], neghalf[:, 0:1], ALU.pow, waits=[t_v_, t_nh])
                        t_o = P.act(obf[s], otm[s], AF.Copy, waits=[t_c, t_r] + obf_free[s], scale=sm[:, 2, i:i + 1])
                        otm_free[s] = [t_o]
                        return [t_sq, t_c], t_o

                    def ep2(i, t_o, h=h, oTs=oTs, siluT=siluT, t_silu=t_silu, ep_last=ep_last):
                        s = i % 2
                        tk = P.tr(psT[s][:, 0:128], obf[s], ident, waits=[t_o] + bankT_free[s], sig=True)
                        obf_free[s] = [tk]
                        sl = slice(i * 128, (i + 1) * 128)
                        t_e = P.stt(oTs[:, sl], psT[s][:, 0:128], gn[:, h:h + 1], siluT[:, sl],
                                    ALU.mult, ALU.mult, waits=[tk, t_gn] + t_silu + [hist_st.get(h - 2)])
                        bankT_free[s] = [t_e]
                        ep_last[0] = t_e

                    gen = P_gen(h + 1) if h + 1 < 8 else None
                    hist_ew[h] = attention(1, [QT], [KT], lambda j, hs=h % 4: V[:, j, hs, :], 128, t_in, elementwise, (ep1, ep2),
                                           pT, pT_free, filler=gen, sbanks=[3, 5])
                    if gen is not None:
                        for _ in gen:
                            pass
                    hist_ep[h] = ep_last[0]
                    hist_st[h] = P.dma("sync", oTd[h * 128:(h + 1) * 128, :], oTs, "ot%d" % (h % 2), waits=[ep_last[0]])
                    if (h + 1) % 4 == 0 and h + 1 < 8:
                        t_v = v_proj((h + 1) // 4)
                if stop_after == ("Bret", layer):
                    bar = barrier()
                    final_tok = [bar]
                    done = True
                    break

                bar = barrier()
                A.reset(baseB2)
                wff = A.alloc([KC, 8], BF16)
                bfc = A.alloc([1], F32)
                sel = A.alloc([8, 128], F32)
                id8 = A.alloc([8], F32)
                ones = A.alloc([T], F32)
                lf = A.alloc([T], F32)
                Fc = A.alloc([T], F32)
                Fs = A.alloc([TB, 8], F32)
                Fb = A.alloc([8, TB], F32)
                bt = A.alloc([8, TB, TB], F32)
                QT2 = [A.alloc([T], BF16) for _ in range(2)]
                KT2 = [A.alloc([T], BF16) for _ in range(2)]
                V = A.alloc([TB, 4, 130], BF16)
                wrv = wrv0
                vnxt = None
                rl = A.alloc([16], F32)
                obf = [A.alloc([128], BF16) for _ in range(2)]
                oTs2 = [A.alloc([T], BF16) for _ in range(2)]
                obf_free = [[], []]
                hist_st = {}
                t_wff = P.dma("gpsimd", wff, w_in_r[:, :, 7168:7176], "wff", waits=bar)
                t_bf = P.dma("sync", bfc[0:8, :], even_b_f, "bf", waits=bar)
                t_sel = P.dma("sync", sel[0:8, :, :], c_sel, "sel", waits=bar)
                t_i8 = P.dma("sync", id8[0:8, :], c_id8, "id8", waits=bar)
                t_on = P.memset("gpsimd", ones[0:8, :], 1.0, waits=bar)
                t_v1 = P.memset("gpsimd", V[:, :, :, 128:130], 1.0, waits=bar)
                for tq in range(4):
                    b = tq % 2
                    for kc in range(KC):
                        tk = P.mm(psF[b][0:8, :], wff[:, kc, :], xnT[:, kc, tq * 512:(tq + 1) * 512],
                                  kc == 0, kc == KC - 1, waits=[t_wff] + bar + bankF_free[b], sig=(kc == KC - 1))
                    t_e = P.act(lf[0:8, tq * 512:(tq + 1) * 512], psF[b][0:8, :], AF.Sigmoid, waits=[tk, t_bf],
                                bias=bfc[0:8, 0:1])
                    bankF_free[b] = [t_e]
                t_ln = P.act(lf[0:8, :], lf[0:8, :], AF.Ln, waits=bankF_free[0] + bankF_free[1])
                t_F = P.op("vector", lambda e: e.tensor_tensor_scan(out=Fc[0:8, :], data0=ones[0:8, :], data1=lf[0:8, :],
                                                                    initial=0.0, op0=ALU.mult, op1=ALU.add),
                           waits=[t_ln, t_on])
                for j in range(TB):
                    tk = P.op("tensor", lambda e, j=j: e.transpose(out=psF[2][:, j * 8:(j + 1) * 8],
                                                                   in_=Fc[0:8, j * 128:(j + 1) * 128], identity=id8[0:8, :]),
                              waits=[t_F, t_i8] + bar, sig=(j == TB - 1))
                t_fs = P.cp("vector", Fs, psF[2][:, 0:128].rearrange("p (j h) -> p j h", h=8), waits=[tk])
                for h in range(8):
                    tk = P.mm(psF[3][:, h * 16:(h + 1) * 16], sel[0:8, h, :],
                              Fc[0:8, :].rearrange("p (i c) -> p i c", c=128)[:, :, 0], True, True,
                              waits=[t_F, t_sel] + bar, sig=(h == 7))
                t_fb = P.cp("vector", Fb, psF[3][:, 0:128].rearrange("p (h i) -> p h i", i=16), waits=[tk])
                t_bt = []
                for j in range(TB):
                    t_bt.append(P.tt("gpsimd" if j % 2 == 0 else "vector", bt[:, :, j, :], Fb,
                                     Fs[:, j, :].unsqueeze(2).to_broadcast([128, 8, TB]), ALU.subtract, waits=[t_fs, t_fb]))
                t_bt = t_bt[-2:]
                nxt = None
                colsf = [4096, 5120]
                pw = {}
                pst = {}

                def load_w(h):
                    pw[h] = [wr.load(w_in_r[:, :, c0 + h * 128:c0 + (h + 1) * 128], bar) for c0 in colsf]

                vpre = {}

                def v_proj(g):
                    k_v, w_v, t_wv = vpre[g] if g in vpre else wrv.load(w_in_r[:, :, 6144 + g * 512:6144 + (g + 1) * 512], bar)
                    tk = proj_tm(w_v, 512, xnT, [t_wv] + bar,
                                 lambda bank, tb, waits: P.cp("vector", V[:, tb, :, 0:128],
                                                              bank[:, 0:512].rearrange("p (g d) -> p g d", g=4), waits=waits))
                    wrv.release(k_v, [tk])
                    if g + 1 < 2:
                        vpre[g + 1] = wrv.load(w_in_r[:, :, 6144 + (g + 1) * 512:6144 + (g + 2) * 512], bar)
                    return bankF_free[0] + bankF_free[1]

                def P_gen(h):
                    (k_q, w_q, t_wq), (k_k, w_k, t_wk) = pw[h]
                    toks = []
                    for dst, wsl, t_w, kk in ((QT2[h % 2], w_q, t_wq, k_q), (KT2[h % 2], w_k, t_wk, k_k)):
                        for tq in range(4):
                            b = tq % 2
                            sl = slice(tq * 512, (tq + 1) * 512)
                            for kc in range(KC):
                                tk = P.mm(psF[b][:, :], wsl[:, kc, 0:128], xnT[:, kc, sl],
                                          kc == 0, kc == KC - 1, waits=[t_w] + bar + bankF_free[b], sig=(kc == KC - 1))
                            t_e = P.cp("vector", dst[:, sl], psF[b], waits=[tk])
                            bankF_free[b] = [t_e]
                            toks.append(t_e)
                            yield
                        wr.release(kk, [tk])
                    if h + 1 < 8:
                        load_w(h + 1)
                    pst[h] = toks[-1:]

                load_w(0)
                t_v = v_proj(0)
                for _ in P_gen(0):
                    pass
                for h in range(8):
                    oTs = oTs2[h % 2]
                    QT, KT = QT2[h % 2], KT2[h % 2]
                    t_in = pst[h] + t_v + [t_bt, t_v1, t_cm]

                    def elementwise(m, i, jtop, n, ps, pt, waits, h=h):
                        toks = []
                        for c in range(n):
                            jj = jtop - c
                            t_e = P.act(pt[:, c * 128:(c + 1) * 128], ps[:, c * 128:(c + 1) * 128], AF.Exp,
                                        waits=waits, scale=SCALE, bias=bt[:, h, jj, i:i + 1])
                            if jj == i:
                                toks.append(P.tt("vector", pt[:, c * 128:(c + 1) * 128], pt[:, c * 128:(c + 1) * 128],
                                                 cm[:, 0, :], ALU.mult, waits=[t_e]))
                        toks.append(t_e)
                        return toks

                    ep_last = [None]

                    def ep1(i, lasts):
                        s = i % 2
                        bidx = 2 + 2 * (i % 2)
                        t_r = P.recip(rl[:, i:i + 1], psF[bidx][:, 128:129], waits=lasts)
                        t_o = P.ts("vector", obf[s], psF[bidx][:, 0:128], rl[:, i:i + 1], None, ALU.mult,
                                   waits=[t_r] + lasts + obf_free[s])
                        return [t_o], t_o

                    def ep2(i, t_o, h=h, oTs=oTs, ep_last=ep_last):
                        s = i % 2
                        tk = P.tr(psT[s][:, 0:128], obf[s], ident, waits=[t_o] + bankT_free[s], sig=True)
                        obf_free[s] = [tk]
                        t_e = P.cp("vector", oTs[:, i * 128:(i + 1) * 128], psT[s][:, 0:128], waits=[tk, hist_st.get(h - 2)])
                        bankT_free[s] = [t_e]
                        ep_last[0] = t_e

                    gen = P_gen(h + 1) if h + 1 < 8 else None
                    attention(1, [QT], [KT], lambda j, hs=h % 4: V[:, j, hs, 0:129], 129, t_in, elementwise, (ep1, ep2), pT, pT_free,
                              filler=gen, sbanks=[3, 5])
                    if gen is not None:
                        for _ in gen:
                            pass
                    hist_st[h] = P.dma("sync", oTd[(8 + h) * 128:(9 + h) * 128, :], oTs, "ot%d" % (h % 2), waits=[ep_last[0]])
                    if (h + 1) % 4 == 0 and h + 1 < 8:
                        t_v = v_proj((h + 1) // 4)
            else:
                lamv = A.alloc([4, 128], F32)
                lsm = A.alloc([8], F32)
                gsub = A.alloc([256], F32)
                QT2 = [[A.alloc([T], BF16) for _ in range(2)] for _ in range(2)]
                KT2 = [[A.alloc([T], BF16) for _ in range(2)] for _ in range(2)]
                pbank = psT[1].bitcast(F32)
                V = V0
                wrv = wrv0
                vnxt = None
                rl = A.alloc([3, 16], F32)
                o1 = [A.alloc([256], F32) for _ in range(2)]
                o2 = [A.alloc([256], F32) for _ in range(2)]
                sq = A.alloc([256], F32)
                obf = [A.alloc([256], BF16) for _ in range(2)]
                oTs2 = [A.alloc([2, T], BF16) for _ in range(2)]
                buf_free = [[], []]
                hist_st = {}
                t_lam = P.dma("sync", lamv.rearrange("p a b -> p (a b)"),
                              odd_lam.rearrange("a b -> (a b)").partition_broadcast(128), "lam", waits=bar)
                t_gs = P.dma("sync", gsub, odd_subln_g.partition_broadcast(128), "gs", waits=bar)
                t_v1 = P.memset("gpsimd", V[:, :, :, 256:258], 1.0, waits=bar)
                t_a = P.stt(sq[:, 0:128], lamv[:, 0, :], 1.0, lamv[:, 1, :], ALU.mult, ALU.mult, waits=[t_lam], accum_out=lsm[:, 0:1])
                t_b = P.stt(sq[:, 128:256], lamv[:, 2, :], 1.0, lamv[:, 3, :], ALU.mult, ALU.mult, waits=[t_lam], accum_out=lsm[:, 1:2])
                t_e1 = P.act(lsm[:, 2:4], lsm[:, 0:2], AF.Exp, waits=[t_a, t_b])
                t_d = P.tt("gpsimd", lsm[:, 4:5], lsm[:, 3:4], lsm[:, 2:3], ALU.subtract, waits=[t_e1])
                t_nl = P.ts("vector", lsm[:, 5:6], lsm[:, 4:5], -LAMBDA_INIT, None, ALU.add, waits=[t_d])
                t_g2 = P.ts("gpsimd", gsub, gsub, 1.0 - LAMBDA_INIT, None, ALU.mult, waits=[t_gs])
                colsd = [0, 2048]
                pw = {}
                pst = {}
                pfree = [[]]

                def load_w(h):
                    if h == 0:
                        pw[0] = pre_w
                        return
                    pw[h] = [wr.load(w_in_r[:, :, c0 + h * 256:c0 + (h + 1) * 256], bar) for c0 in colsd]

                vpre = {}

                def v_proj(g):
                    k_v, w_v, t_wv = pre_v if g == 0 else vpre[g]
                    tk = proj_tm(w_v, 512, xnT, [t_wv] + bar,
                                 lambda bank, tb, waits: P.cp("vector", V[:, tb, :, 0:256],
                                                              bank[:, 0:512].rearrange("p (g d) -> p g d", g=2), waits=waits))
                    wrv.release(k_v, [tk])
                    if g + 1 < 4:
                        vpre[g + 1] = wrv.load(w_in_r[:, :, 4096 + (g + 1) * 512:4096 + (g + 2) * 512], bar)
                    return bankF_free[0] + bankF_free[1]

                def P_gen(h):
                    (k_q, w_q, t_wq), (k_k, w_k, t_wk) = pw[h]
                    toks = []
                    for dsts, wsl, t_w, kk in ((QT2[h % 2], w_q, t_wq, k_q), (KT2[h % 2], w_k, t_wk, k_k)):
                        for m in range(2):
                            for tq in range(4):
                                sl = slice(tq * 512, (tq + 1) * 512)
                                for kc in range(KC):
                                    tk = P.mm(pbank, wsl[:, kc, m * 128:(m + 1) * 128], xnT[:, kc, sl],
                                              kc == 0, kc == KC - 1, waits=[t_w] + bar + pfree[0], sig=(kc == KC - 1))
                                t_e = P.act(dsts[m][:, sl], pbank, AF.Copy, waits=[tk])
                                pfree[0] = [t_e]
                                toks.append(t_e)
                                yield
                        wr.release(kk, [tk])
                    if h + 1 < 8:
                        load_w(h + 1)
                    pst[h] = toks[-1:]

                load_w(0)
                t_v = t_v_pre
                vpre[1] = pre_vnext
                for _ in P_gen(0):
                    pass
                for h in range(8):
                    oTs = oTs2[h % 2]
                    QT, KT = QT2[h % 2], KT2[h % 2]
                    t_in = pst[h] + t_v + [t_v1, t_cm, t_nl, t_g2]

                    def elementwise(m, i, jtop, n, ps, pt, waits):
                        t_e = P.act(pt[:, 0:n * 128], ps[:, 0:n * 128], AF.Exp, waits=waits, scale=SCALE)
                        if jtop == i:
                            return [t_e, P.tt("vector", pt[:, 0:128], pt[:, 0:128], cm[:, 1, :], ALU.mult, waits=[t_e])]
                        return [t_e]

                    ep_last = [None]

                    def ep1(i, lasts):
                        s = i % 2
                        b1 = 2 + 2 * (i % 2)
                        b2 = b1 + 1
                        t_r1 = P.recip(rl[:, 0, i:i + 1], psF[b1][:, 256:257], waits=lasts)
                        t_r2 = P.recip(rl[:, 1, i:i + 1], psF[b2][:, 256:257], waits=lasts)
                        t_r3 = P.ts("vector", rl[:, 2, i:i + 1], rl[:, 1, i:i + 1], lsm[:, 5:6], None, ALU.mult, waits=[t_r2, t_nl])
                        t_1 = P.ts("vector", o1[s], psF[b1][:, 0:256], rl[:, 0, i:i + 1], None, ALU.mult,
                                   waits=[t_r1] + lasts + buf_free[s])
                        t_2 = P.stt(o2[s], psF[b2][:, 0:256], rl[:, 2, i:i + 1], o1[s], ALU.mult, ALU.add, waits=[t_1, t_r3] + lasts)
                        t_3 = P.stt(sq, o2[s], 1.0, o2[s], ALU.mult, ALU.mult, waits=[t_2], accum_out=sm[:, 0, i:i + 1])
                        t_r = rstd_of(sm[:, 0, i:i + 1], sm[:, 1, i:i + 1], sm[:, 2, i:i + 1], 256, GN_EPS, [t_3])
                        t_o = P.stt(obf[s], o2[s], sm[:, 2, i:i + 1], gsub, ALU.mult, ALU.mult, waits=[t_r, t_g2])
                        return [t_1, t_2], t_o

                    def ep2(i, t_o, h=h, oTs=oTs, ep_last=ep_last):
                        s = i % 2
                        for c in range(2):
                            tk = P.tr(psT[0][:, s * 256 + c * 128:s * 256 + (c + 1) * 128], obf[s][:, c * 128:(c + 1) * 128], ident,
                                      waits=[t_o] + bankT_free[s], sig=(c == 1))
                        buf_free[s] = [tk]
                        t_e = P.cp("vector", oTs[:, :, i * 128:(i + 1) * 128],
                                   psT[0][:, s * 256:(s + 1) * 256].rearrange("p (c t) -> p c t", c=2), waits=[tk, hist_st.get(h - 2)])
                        bankT_free[s] = [t_e]
                        ep_last[0] = t_e

                    gen = P_gen(h + 1) if h + 1 < 8 else None
                    attention(2, QT, KT, lambda j, hs=h % 2: V[:, j, hs, 0:257], 257, t_in, elementwise, (ep1, ep2), pT, pT_free,
                              filler=gen)
                    if gen is not None:
                        for _ in gen:
                            pass
                    hist_st[h] = P.dma("sync", oTd[h * 256:(h + 1) * 256, :].rearrange("(c p) t -> p c t", p=128), oTs,
                                       "ot%d" % (h % 2), waits=[ep_last[0]])
                    if (h + 1) % 2 == 0 and h + 1 < 8:
                        t_v = v_proj((h + 1) // 2)
            if stop_after == ("B", layer):
                bar = barrier()
                final_tok = [bar]
                done = True
                break

            bar = barrier()
            A.reset()
            oT = A.alloc([KC, T], BF16)
            wo = WRing(2, [KC, 512], "wo")
            hs = [A.alloc([TB, 512], F32) for _ in range(2)]
            hs_free = [[], []]
            oTd_r = oTd.rearrange("(kc p) t -> p kc t", p=128)
            t_otq = [P.dma("sync", oT[:, :, q * 512:(q + 1) * 512], oTd_r[:, :, q * 512:(q + 1) * 512], "oT%d" % q, waits=bar)
                     for q in range(4)]
            w_out_r = w_out.rearrange("(kc p) n -> p kc n", p=128)
            bi = 0
            def c1_load(nb):
                s = nb % 2
                return (wo.load(w_out_r[:, :, nb * 512:(nb + 1) * 512], bar),
                        P.dma("sync", hs[s], src_h[:, nb * 512:(nb + 1) * 512].rearrange("(tb p) n -> p tb n", p=128),
                              "hs%d" % s, waits=bar + hs_free[s]))
            c1n = c1_load(0)
            for nb in range(4):
                s = nb % 2
                (k_w, wsl, t_w), t_h = c1n
                if nb + 1 < 4:
                    c1n = c1_load(nb + 1)
                adds = []
                for tb in range(TB):
                    b = bi % 4
                    bi += 1
                    for kc in range(KC):
                        tk = P.mm(psF[b][:, :], oT[:, kc, tb * 128:(tb + 1) * 128], wsl[:, kc, :],
                                  kc == 0, kc == KC - 1, waits=[t_otq[tb // 4], t_w] + bar + bankF_free[b], sig=(kc == KC - 1))
                    t_a = P.tt("vector", hs[s][:, tb, :], hs[s][:, tb, :], psF[b][:, :], ALU.add, waits=[tk, t_h])
                    bankF_free[b] = [t_a]
                    adds.append(t_a)
                wo.release(k_w, [tk])
                t_st = P.dma("sync", mid_h[:, nb * 512:(nb + 1) * 512].rearrange("(tb p) n -> p tb n", p=128), hs[s],
                             "hso%d" % s, waits=[adds[-1]])
                hs_free[s] = [t_st]
            if stop_after == ("C1", layer):
                bar = barrier()
                final_tok = [bar]
                done = True
                break

            last_layer = (layer == nlayers - 1)
            bar = barrier()
            A.reset()
            hacc = A.alloc([8, D], F32)
            xn2T = A.alloc([KC, 1024], BF16)
            g_bc = A.alloc([D], F32)
            w1r = WRing(2, [KC, 512], "w1r")
            w2r = WRing(2, [4, D], "w2r")
            aT = [A.alloc([4, 1024], BF16) for _ in range(2)]
            rtmp = [A.alloc([512], F32) for _ in range(2)]
            gf = A.alloc([D], F32)
            xs_view = gf.bitcast(BF16).rearrange("p (a b) -> p a b", a=2)
            tmpn = {"junk": aT[0][:, 0:2, :].rearrange("p a b -> p (a b)"), "xs": xs_view,
                    "ss": A.alloc([TB], F32), "v": A.alloc([TB], F32), "rs": A.alloc([TB], F32)}
            t_g = P.dma("sync", g_bc, norm_mlp_g[layer:layer + 1, :].partition_broadcast(128), "gbc", waits=bar)
            w1_r = mlp_w1[layer].rearrange("(kc p) f -> p kc f", p=128)
            w2_r = mlp_w2[layer].rearrange("(fc p) n -> p fc n", p=128)
            NFG = DFF // 512

            w1l = {}
            w2l = {}

            def issue_w1(idx):
                if idx < 2 * NFG:
                    fg_ = idx % NFG
                    w1l[idx] = w1r.load(w1_r[:, :, fg_ * 512:(fg_ + 1) * 512], bar)

            def issue_w2(idx):
                if idx < 2 * NFG:
                    fg_ = idx % NFG
                    w2l[idx] = w2r.load(w2_r[:, fg_ * 4:(fg_ + 1) * 4, :], bar)
            issue_w1(0)
            issue_w2(0)
            issue_w1(1)
            issue_w2(1)
            st_tok = {}
            xs_extra = []
            aT_free = [[], []]
            rtmp_free = [[], []]
            cnt = {"ri": 0, "yb": 0}
            hst = {}

            def H(idx):
                fg = idx % NFG
                a = fg % 2
                k1, w1s, t_w1 = w1l[idx]
                sq_toks = []
                for fc in range(4):
                    for th in range(2):
                        b = (fc * 2 + th) % 2
                        for kc in range(KC):
                            tk = P.mm(psF[b][:, :], w1s[:, kc, fc * 128:(fc + 1) * 128],
                                      xn2T[:, kc, th * 512:(th + 1) * 512], kc == 0, kc == KC - 1,
                                      waits=[t_w1] + hst["xn_all"] + bar + bankF_free[b], sig=(kc == KC - 1))
                        r = cnt["ri"] % 2
                        cnt["ri"] += 1
                        t_r = P.act(rtmp[r], psF[b][:, :], AF.Relu, waits=[tk] + rtmp_free[r])
                        bankF_free[b] = [t_r]
                        t_s = P.tt("gpsimd", aT[a][:, fc, th * 512:(th + 1) * 512], rtmp[r], rtmp[r], ALU.mult,
                                   waits=[t_r] + aT_free[a])
                        rtmp_free[r] = [t_s]
                        sq_toks.append(t_s)
                w1r.release(k1, [tk])
                issue_w1(idx + 2)
                hst[("sq", idx)] = sq_toks[-1]

            def Y(idx):
                half, fg = idx // NFG, idx % NFG
                a = fg % 2
                k2, w2s, t_w2 = w2l[idx]
                t_sq = hst[("sq", idx)]
                xn_all = hst["xn_all"]
                hacc_tok = hst["hacc_tok"]
                pend_fn = None
                for tb in range(8):
                    for nb in range(4):
                        b = 2 + (cnt["yb"] % 4)
                        cnt["yb"] += 1
                        for fc in range(4):
                            tk = P.mm(psF[b][:, :], aT[a][:, fc, tb * 128:(tb + 1) * 128],
                                      w2s[:, fc, nb * 512:(nb + 1) * 512], fc == 0, fc == 3,
                                      waits=[t_w2, t_sq] + bar + bankF_free[b], sig=(fc == 3))
                        t_a = P.tt("vector", hacc[:, tb, nb * 512:(nb + 1) * 512], hacc[:, tb, nb * 512:(nb + 1) * 512],
                                   psF[b][:, :], ALU.add, waits=[tk] + hacc_tok[tb] + xn_all)
                        bankF_free[b] = [t_a]
                    if fg == NFG - 1:
                        r0 = half * 1024 + tb * 128
                        if not last_layer:
                            st_tok[tb] = P.dma("sync", dst_h[r0:r0 + 128, :], hacc[:, tb, :], "hst%d" % tb, waits=[t_a])
                        else:
                            c = 8 + tb
                            t_ss = P.act(tmpn["junk"], hacc[:, tb, :], AF.Square, waits=[t_a], accum_out=tmpn["ss"][:, c:c + 1])
                            t_rr = rstd_of(tmpn["ss"][:, c:c + 1], tmpn["v"][:, c:c + 1], tmpn["rs"][:, c:c + 1], D, RMS_EPS, [t_ss])
                            if pend_fn is not None:
                                ptb, pc, pt_r, pr0 = pend_fn
                                t_x = P.stt(hacc[:, ptb, :], hacc[:, ptb, :], tmpn["rs"][:, pc:pc + 1], gf, ALU.mult, ALU.mult,
                                            waits=[pt_r, hst["t_gf"]])
                                st_tok[ptb] = P.dma("sync", y[pr0:pr0 + 128, :], hacc[:, ptb, :], "yo%d" % ptb, waits=[t_x])
                            pend_fn = (tb, c, t_rr, r0)
                if pend_fn is not None:
                    ptb, pc, pt_r, pr0 = pend_fn
                    t_x = P.stt(hacc[:, ptb, :], hacc[:, ptb, :], tmpn["rs"][:, pc:pc + 1], gf, ALU.mult, ALU.mult,
                                waits=[pt_r, hst["t_gf"]])
                    st_tok[ptb] = P.dma("sync", y[pr0:pr0 + 128, :], hacc[:, ptb, :], "yo%d" % ptb, waits=[t_x])
                    hst["xs_extra"] = [t_x]
                hst["hacc_tok"] = [[t_a] for _ in range(8)]
                aT_free[a] = [tk]
                w2r.release(k2, [tk])
                issue_w2(idx + 2)

            hst["xs_extra"] = []
            for half in range(2):
                t_hl = []
                for tb in range(8):
                    r0 = half * 1024 + tb * 128
                    t_hl.append(P.dma("sync", hacc[:, tb, :], mid_h[r0:r0 + 128, :], "hacc%d" % tb,
                                      waits=bar + [st_tok.get(tb)]))
                xs_free = [list(hst["xs_extra"]), list(hst["xs_extra"])]
                norm_pipeline(8, lambda bq: (hacc[:, bq, :], [t_hl[bq]] + bar), g_bc, t_g, xn2T, tmpn, xs_free)
                hst["xn_all"] = bankT_free[0] + bankT_free[1]
                if last_layer:
                    hst["t_gf"] = P.dma("sync", gf, final_g.partition_broadcast(128), "gf", waits=xs_free[0] + xs_free[1])
                hst["hacc_tok"] = [[t] for t in t_hl]
                H(half * NFG)
                for fg in range(NFG):
                    if fg + 1 < NFG:
                        H(half * NFG + fg + 1)
                    Y(half * NFG + fg)
        if not done:
            bar = barrier()
            final_tok = list(bar)
        P.emit(P._flat(final_tok, []))
    return nc


_NC_CACHE = {}


def _consts():
    half = 64
    inv = (10000.0 ** (-np.arange(half, dtype=np.float32) / half)).astype(np.float32)
    pos = np.arange(T, dtype=np.float32)
    ang = (pos[:, None] * inv[None, :]).astype(np.float32)
    cos = np.cos(ang).astype(np.float32).T
    sin = np.sin(ang).astype(np.float32).T
    rope = np.zeros((128, 2, T), np.float32)
    rope[0:64, 0], rope[64:128, 0] = cos, cos
    rope[0:64, 1], rope[64:128, 1] = sin, sin
    rot = np.zeros((128, 128), np.float32)
    for dp in range(64):
        rot[dp + 64, dp] = -1.0
        rot[dp, dp + 64] = 1.0
    dec = np.zeros((8, 128, T), np.float32)
    sl = np.arange(128, dtype=np.float64)[:, None]
    tt = np.arange(T, dtype=np.float64)[None, :]
    for h in range(8):
        log_g = np.log1p(-(2.0 ** (-5.0 - h)))
        m = np.exp(log_g * (tt - sl))
        d = np.exp(log_g * np.abs(tt[:, :128] - sl))
        cs_, ct_ = (sl // 64), (tt[:, :128] // 64)
        diag = np.where(cs_ == ct_, d, np.where(cs_ < ct_, m[:, :128], 0.0))
        m[:, :128] = diag
        dec[h] = (m * (HD ** -0.5)).astype(np.float32)
    mask = np.zeros((128, 2, 128), np.float32)
    s_ = np.arange(128)[:, None]
    t_ = np.arange(128)[None, :]
    mask[:, 0, :] = (s_ <= t_)
    mask[:, 1, :] = ~((s_ >= 64) & (t_ < 64))
    sel = np.zeros((8, 8, 128), np.float32)
    for h in range(8):
        sel[h, h, :] = 1.0
    return {"c_rope": rope, "c_rot": rot, "c_dec": dec, "c_mask": mask, "c_sel": sel,
            "c_id8": np.eye(8, dtype=np.float32)}


def make_in_maps(inputs):
    f = lambda a: np.ascontiguousarray(np.asarray(a, dtype=np.float32))
    shared = {
        "norm_mix_g": f(inputs["norm_mix_g"]),
        "norm_mlp_g": f(inputs["norm_mlp_g"]),
        "even_w_in": f(inputs["even_w_in"])[0],
        "even_b_f": f(inputs["even_b_f"]).reshape(8, 1),
        "even_ret_gn": np.ascontiguousarray(f(inputs["even_ret_gn"]).reshape(8, 128).T),
        "even_w_out": f(inputs["even_w_out"])[0],
        "odd_w_in": f(inputs["odd_w_in"])[0],
        "odd_lam": np.ascontiguousarray(np.concatenate([f(inputs["odd_lambda_q1"]), f(inputs["odd_lambda_k1"]),
                                                        f(inputs["odd_lambda_q2"]), f(inputs["odd_lambda_k2"])], axis=0)),
        "odd_subln_g": f(inputs["odd_subln_g"]).reshape(1, 256),
        "odd_w_out": f(inputs["odd_w_out"])[0],
        "mlp_w1": f(inputs["mlp_w1"]),
        "mlp_w2": f(inputs["mlp_w2"]),
        "final_g": f(inputs["final_g"]).reshape(1, D),
    }
    shared.update(_consts())
    xs = f(inputs["x"])
    return [dict(shared, x=xs[b]) for b in range(xs.shape[0])]


def kernel(**inputs):
    if "nc" not in _NC_CACHE:
        _NC_CACHE["nc"] = build_program()
    nc = _NC_CACHE["nc"]
    in_maps = make_in_maps(inputs)
    res = run_bass_kernel_spmd(nc, in_maps, core_ids=list(range(len(in_maps))))
    return np.stack([np.asarray(r["y"], dtype=np.float32) for r in res.results], axis=0)
```

```python
import contextlib
import math
import numpy as np
import concourse.bass as bass
import concourse.mybir as mybir
from concourse.bass_utils import run_bass_kernel_spmd

F32 = mybir.dt.float32
BF16 = mybir.dt.bfloat16
AF = mybir.ActivationFunctionType
ALU = mybir.AluOpType
AX = mybir.AxisListType

D = 2048
T = 2048
KC = 16
TB = 16
HD = 128
DFF = 8192
EVEN_IN = 7176
ODD_IN = 6144
RMS_EPS = 1e-6
GN_EPS = 1e-5
LAMBDA_INIT = 0.8 - 0.6 * math.exp(-0.3 * 1)
SCALE = HD ** -0.5

ENGS = ["sync", "scalar", "gpsimd", "vector", "tensor"]


class Prog:
    def __init__(self, nc):
        self.nc = nc
        self.q = {e: [] for e in ENGS}
        self.cnt = {e: 0 for e in ENGS}
        self.dsem = {}
        self.seen = {e: {} for e in ENGS}
        self.dtoks = {}

    def _flat(self, waits, out):
        for w in waits:
            if w is None:
                continue
            if isinstance(w, list):
                self._flat(w, out)
            else:
                out.append(w)
        return out

    def _ws(self, eng, waits):
        ws = []
        for key, val in self._flat(waits, []):
            if self.seen[eng].get(key, 0) >= val:
                continue
            self.seen[eng][key] = val
            ws.append((key, val))
        return ws

    def op(self, eng, fn, waits=(), sig=True):
        ws = self._ws(eng, waits)
        tok = None
        if sig:
            self.cnt[eng] += 1
            tok = (("p", eng), self.cnt[eng])
        self.q[eng].append((fn, ws, ("p", eng) if sig else None, 1))
        return tok

    def dma(self, eng, out, in_, sem, waits=()):
        ws = self._ws(eng, waits)
        self.dsem.setdefault(sem, 0)
        self.dsem[sem] += 16
        tok = (("d", sem), self.dsem[sem])
        self.q[eng].append((lambda e: e.dma_start(out=out, in_=in_), ws, ("d", sem), 16))
        self.dtoks[sem] = tok
        return tok

    def mm(self, out, lhsT, rhs, start, stop, waits=(), sig=False):
        return self.op("tensor", lambda e: e.matmul(out, lhsT=lhsT, rhs=rhs, start=start, stop=stop), waits, sig)

    def tr(self, out, in_, ident, waits=(), sig=False):
        return self.op("tensor", lambda e: e.transpose(out=out, in_=in_, identity=ident), waits, sig)

    def act(self, out, in_, func, waits=(), sig=True, **kw):
        return self.op("scalar", lambda e: e.activation(out=out, in_=in_, func=func, **kw), waits, sig)

    def tt(self, eng, out, in0, in1, op, waits=(), sig=True):
        return self.op(eng, lambda e: e.tensor_tensor(out=out, in0=in0, in1=in1, op=op), waits, sig)

    def ts(self, eng, out, in0, s1, s2, op0, op1=None, waits=(), sig=True):
        if op1 is None:
            return self.op(eng, lambda e: e.tensor_scalar(out=out, in0=in0, scalar1=s1, scalar2=None, op0=op0), waits, sig)
        return self.op(eng, lambda e: e.tensor_scalar(out=out, in0=in0, scalar1=s1, scalar2=s2, op0=op0, op1=op1), waits, sig)

    def stt(self, out, in0, scalar, in1, op0, op1, waits=(), sig=True, accum_out=None):
        if accum_out is None:
            return self.op("vector", lambda e: e.scalar_tensor_tensor(out=out, in0=in0, scalar=scalar, in1=in1, op0=op0, op1=op1), waits, sig)
        return self.op("vector", lambda e: e.scalar_tensor_tensor(out=out, in0=in0, scalar=scalar, in1=in1, op0=op0, op1=op1, accum_out=accum_out), waits, sig)

    def recip(self, out, in_, waits=(), sig=True):
        return self.op("vector", lambda e: e.reciprocal(out=out, in_=in_), waits, sig)

    def memset(self, eng, ap, val, waits=(), sig=True):
        return self.op(eng, lambda e: e.memset(ap, val), waits, sig)

    def cp(self, eng, out, in_, waits=(), sig=True):
        if eng == "scalar":
            return self.op(eng, lambda e: e.copy(out=out, in_=in_), waits, sig)
        return self.op(eng, lambda e: e.tensor_copy(out=out, in_=in_), waits, sig)

    def barrier(self, scr, skip=None):
        toks = []
        toks.append(self.op("vector", lambda e: e.memset(scr[0:1, 0:1], 0.0)))
        toks.append(self.op("gpsimd", lambda e: e.memset(scr[0:1, 1:2], 0.0)))
        toks.append(self.op("scalar", lambda e: e.copy(out=scr[0:1, 3:4], in_=scr[0:1, 2:3])))
        if self.cnt["tensor"] > 0:
            toks.append((("p", "tensor"), self.cnt["tensor"]))
        toks += [t for s, t in self.dtoks.items() if not (skip and s.startswith(skip))]
        self.bar_toks = toks
        return toks

    def emit(self, final_waits):
        nc = self.nc
        with contextlib.ExitStack() as st:
            sems = {}
            for e in ENGS:
                sems[("p", e)] = st.enter_context(nc.semaphore("p_" + e))
            for s in self.dsem:
                sems[("d", s)] = st.enter_context(nc.semaphore("d_" + s))
            block = st.enter_context(nc.Block())

            def runner(ename):
                def _(eng):
                    for fn, ws, sg, inc in self.q[ename]:
                        for key, val in ws:
                            eng.wait_ge(sems[key], val)
                        ins = fn(eng)
                        if sg is not None:
                            ins.then_inc(sems[sg], inc)
                    if ename == "sync":
                        for key, val in final_waits:
                            eng.wait_ge(sems[key], val)
                return _

            block.sync(runner("sync"))
            block.scalar(runner("scalar"))
            block.gpsimd(runner("gpsimd"))
            block.vector(runner("vector"))
            block.tensor(runner("tensor"))


class Arena:
    def __init__(self, tensor, nbytes):
        self.t = tensor
        self.nbytes = nbytes
        self.off = 0
        self.base = 0

    def reset(self, base=None):
        self.off = self.base if base is None else base

    def alloc(self, shape, dt):
        esz = 4 if dt == F32 else 2
        n = int(np.prod(shape))
        nb = (n * esz + 31) // 32 * 32
        assert self.off + nb <= self.nbytes, ("SBUF arena overflow", self.off, nb, self.nbytes)
        w0 = self.off // 4
        w1 = (self.off + nb) // 4
        self.off += nb
        ap = self.t[:, w0:w1]
        if dt != F32:
            ap = ap.bitcast(dt)
        ap = ap[:, 0:n]
        if len(shape) == 2:
            ap = ap.rearrange("p (a b) -> p a b", b=shape[1])
        elif len(shape) == 3:
            ap = ap.rearrange("p (a b c) -> p a b c", b=shape[1], c=shape[2])
        elif len(shape) == 4:
            ap = ap.rearrange("p (a b c d) -> p a b c d", b=shape[1], c=shape[2], d=shape[3])
        return ap


class Ring:
    def __init__(self, bufs):
        self.bufs = bufs
        self.n = len(bufs)
        self.i = 0
        self.free = [[] for _ in bufs]

    def next(self):
        k = self.i % self.n
        self.i += 1
        return k, self.bufs[k], self.free[k]

    def release(self, k, toks):
        self.free[k] = [t for t in toks if t is not None]


ARENA_BYTES = 200 * 1024


def build_program(stop_after=None, nlayers=2, debug=False):
    nc = bass.Bass("TRN2", target_bir_lowering=False)

    def din(name, shape, dt=F32):
        return nc.dram_tensor(name, list(shape), dt, kind="ExternalInput").ap()

    x = din("x", [T, D])
    norm_mix_g = din("norm_mix_g", [2, D])
    norm_mlp_g = din("norm_mlp_g", [2, D])
    even_w_in = din("even_w_in", [D, EVEN_IN])
    even_b_f = din("even_b_f", [8, 1])
    even_ret_gn = din("even_ret_gn", [128, 8])
    even_w_out = din("even_w_out", [D, D])
    odd_w_in = din("odd_w_in", [D, ODD_IN])
    odd_lam = din("odd_lam", [4, 128])
    odd_subln_g = din("odd_subln_g", [1, 256])
    odd_w_out = din("odd_w_out", [D, D])
    mlp_w1 = din("mlp_w1", [2, D, DFF])
    mlp_w2 = din("mlp_w2", [2, DFF, D])
    final_g = din("final_g", [1, D])
    c_rope = din("c_rope", [128, 2, T])
    c_rot = din("c_rot", [128, 128])
    c_dec = din("c_dec", [8, 128, T])
    c_mask = din("c_mask", [128, 2, 128])
    c_sel = din("c_sel", [8, 8, 128])
    c_id8 = din("c_id8", [8, 8])
    y = nc.dram_tensor("y", [T, D], F32, kind="ExternalOutput").ap()
    ikind = "ExternalOutput" if debug else "Internal"
    hA = nc.dram_tensor("hA", [T, D], F32, kind=ikind).ap()
    hB = nc.dram_tensor("hB", [T, D], F32, kind=ikind).ap()
    oTd = nc.dram_tensor("oTd", [D, T], BF16, kind=ikind).ap()

    with contextlib.ExitStack() as st:
        arena_t = st.enter_context(nc.sbuf_tensor("arena", [128, ARENA_BYTES // 4], F32))
        psF = [st.enter_context(nc.psum_tensor("psF%d" % i, [128, 512], F32))[:, :] for i in range(6)]
        psT = [st.enter_context(nc.psum_tensor("psT%d" % i, [128, 1024], BF16))[:, :] for i in range(2)]
        P = Prog(nc)
        A = Arena(arena_t, ARENA_BYTES)

        ident = A.alloc([128], BF16)
        scr = A.alloc([8], F32)
        neghalf = A.alloc([1], F32)
        t0 = P.op("gpsimd", lambda e: e.memset(ident, 0.0))
        t_id = P.op("gpsimd", lambda e: e.affine_select(out=ident, in_=ident, pattern=[[-1, 128]],
                                                        compare_op=ALU.not_equal, fill=1.0, base=0,
                                                        channel_multiplier=1), waits=[t0])
        t1 = P.op("gpsimd", lambda e: e.memset(scr, 0.0))
        t_nh = P.op("gpsimd", lambda e: e.memset(neghalf, -0.5))
        A.base = A.off
        init_toks = [t_id, t1, t_nh]

        bankF_free = [[] for _ in range(6)]
        bankT_free = [[] for _ in range(2)]

        def barrier():
            b = P.barrier(scr)
            for l in bankF_free:
                del l[:]
            for l in bankT_free:
                del l[:]
            return b

        def rstd_of(ss_col, v_col, r_col, n, eps, waits):
            t_v = P.ts("vector", v_col, ss_col, 1.0 / n, eps, ALU.mult, ALU.add, waits=waits)
            return P.tt("gpsimd", r_col, v_col, neghalf[:, 0:1], ALU.pow, waits=[t_v, t_nh])

        def norm_s1a(src, t_src, bi, tmp):
            t_ss = P.act(tmp["junk"], src, AF.Square, waits=t_src + init_toks, accum_out=tmp["ss"][:, bi:bi + 1])
            t_r = rstd_of(tmp["ss"][:, bi:bi + 1], tmp["v"][:, bi:bi + 1], tmp["rs"][:, bi:bi + 1], D, RMS_EPS, [t_ss])
            return t_ss, t_r

        def norm_s1b(src, t_src, t_r, g_bc, t_g, bi, tmp, xs_free):
            s = bi % 2
            return P.stt(tmp["xs"][:, s, :], src, tmp["rs"][:, bi:bi + 1], g_bc, ALU.mult, ALU.mult,
                         waits=[t_r, t_g] + t_src + xs_free[s])

        def norm_pipeline(nblk, get_src, g_bc, t_g, dstT, tmp, xs_free, on_iter=None):
            st = {}
            readers = {}
            for k in range(nblk + 2):
                if on_iter is not None:
                    on_iter(k, readers)
                if k < nblk:
                    src, t_src = get_src(k)
                    t_ss, t_r = norm_s1a(src, t_src, k, tmp)
                    st[k] = [src, t_src, t_ss, t_r, None]
                if 0 <= k - 1 < nblk:
                    src, t_src, t_ss, t_r, _ = st[k - 1]
                    t_x = norm_s1b(src, t_src, t_r, g_bc, t_g, k - 1, tmp, xs_free)
                    st[k - 1][4] = t_x
                    readers[k - 1] = [t_ss, t_x]
                if 0 <= k - 2 < nblk:
                    norm_s2(st[k - 2][4], dstT, (k - 2) * 128, k - 2, tmp, xs_free)
            return readers

        def norm_s2(t_x, dstT, col, bi, tmp, xs_free):
            s = bi % 2
            evs = []
            for kg in range(2):
                bank = psT[kg]
                for j in range(8):
                    kc = kg * 8 + j
                    tk = P.tr(bank[:, j * 128:(j + 1) * 128], tmp["xs"][:, s, kc * 128:(kc + 1) * 128], ident,
                              waits=[t_x, t_id] + bankT_free[kg], sig=(j == 7))
                eng = "scalar" if kg == 0 else "vector"
                t_e = P.cp(eng, dstT[:, kg * 8:(kg + 1) * 8, col:col + 128],
                           bank.rearrange("p (j c) -> p j c", j=8), waits=[tk])
                bankT_free[kg] = [t_e]
                evs.append(t_e)
            xs_free[s] = [tk]
            return evs

        def alloc_norm_tmp():
            return {"junk": A.alloc([D], BF16), "xs": A.alloc([2, D], BF16), "ss": A.alloc([TB], F32),
                    "v": A.alloc([TB], F32), "rs": A.alloc([TB], F32)}

        def proj_fm(wslab, c0, xnT, waits, evac):
            for tq in range(4):
                b = tq % 2
                for kc in range(KC):
                    tk = P.mm(psF[b][:, :], wslab[:, kc, c0:c0 + 128], xnT[:, kc, tq * 512:(tq + 1) * 512],
                              kc == 0, kc == KC - 1, waits=waits + bankF_free[b], sig=(kc == KC - 1))
                bankF_free[b] = [evac(psF[b], tq, [tk])]
            return tk

        def proj_tm(wslab, ncols, xnT, waits, evac):
            for tb in range(TB):
                b = tb % 2
                for kc in range(KC):
                    tk = P.mm(psF[b][:, 0:ncols], xnT[:, kc, tb * 128:(tb + 1) * 128], wslab[:, kc, 0:ncols],
                              kc == 0, kc == KC - 1, waits=waits + bankF_free[b], sig=(kc == KC - 1))
                bankF_free[b] = [evac(psF[b], tb, [tk])]
            return tk

        def attention(nmaps, QTs, KTs, vaug, dvp, t_in, elementwise, epilogue, pT, pT_free, filler=None, sbanks=None):
            items = []
            last_ew = [None]
            for i in range(TB):
                j = i
                while j >= 0:
                    n = min(4, j + 1)
                    items.append((i, j, n, j - n < 0))
                    j -= n
            sidx = 0
            lasts = {}
            deferred = [None]
            if sbanks is None:
                sbanks = [0, 1, 3, 5] if nmaps == 1 else [0, 1]
            depth = 3 if nmaps == 1 else 1
            queue = []

            ep1, ep2 = epilogue

            for k in range(len(items) + 1):
                if k < len(items):
                    i, jtop, n, is_last = items[k]
                    its = []
                    for m in range(nmaps):
                        sb = sbanks[sidx % len(sbanks)]
                        pk = sidx % len(pT)
                        sidx += 1
                        for c in range(n):
                            jj = jtop - c
                            tk = P.mm(psF[sb][:, c * 128:(c + 1) * 128], KTs[m][:, jj * 128:(jj + 1) * 128],
                                      QTs[m][:, i * 128:(i + 1) * 128], True, True,
                                      waits=t_in + bankF_free[sb], sig=(c == n - 1))
                        t_e = elementwise(m, i, jtop, n, psF[sb], pT[pk], [tk] + pT_free[pk])
                        bankF_free[sb] = [t_e]
                        last_ew[0] = t_e
                        its.append((m, pk, t_e))
                    queue.append((i, jtop, n, is_last, its))
                while queue and (len(queue) > depth or k >= len(items)):
                    pi, pjtop, pn, plast, pitems = queue.pop(0)
                    for (m, pk, t_e) in pitems:
                        bidx = 2 + 2 * (pi % 2) + m
                        for c in range(pn):
                            jj = pjtop - c
                            tk = P.mm(psF[bidx][:, 0:dvp], pT[pk][:, c * 128:(c + 1) * 128], vaug(jj),
                                      jj == pi, jj == 0, waits=[t_e] + t_in + bankF_free[bidx], sig=(c == pn - 1))
                        lasts[(pi, m)] = tk
                        pT_free[pk] = [tk]
                    if filler is not None:
                        next(filler, None)
                    if plast:
                        frees, state = ep1(pi, [lasts[(pi, m)] for m in range(nmaps)])
                        for m in range(nmaps):
                            bankF_free[2 + 2 * (pi % 2) + m] = frees
                        if deferred[0] is not None:
                            ep2(*deferred[0])
                        deferred[0] = (pi, state)
            if deferred[0] is not None:
                ep2(*deferred[0])
            return last_ew[0]

        class WRing:
            def __init__(self, n, shape, name):
                self.bufs = [A.alloc(shape, BF16) for _ in range(n)]
                self.free = [[] for _ in range(n)]
                self.i = 0
                self.name = name
                self.n = n

            def load(self, src_ap, extra_waits=(), sub=None):
                k = self.i % self.n
                self.i += 1
                dst = self.bufs[k] if sub is None else sub(self.bufs[k])
                t = P.dma("gpsimd", dst, src_ap, "%s%d" % (self.name, k), waits=list(extra_waits) + self.free[k])
                return k, self.bufs[k], t

            def release(self, k, toks):
                self.free[k] = [t for t in toks if t is not None]

        final_tok = []
        done = False
        for layer in range(nlayers):
            src_h = x if layer == 0 else hB
            mid_h = hA
            dst_h = hB
            w_in = even_w_in if layer == 0 else odd_w_in
            w_out = even_w_out if layer == 0 else odd_w_out
            w_in_r = w_in.rearrange("(kc p) n -> p kc n", p=128)

            bar = barrier()
            A.reset()
            xnT = A.alloc([KC, T], BF16)
            wr = WRing(4, [KC, 128 if layer == 0 else 256], "wr")
            wrv0 = WRing(1, [KC, 512], "wrv")
            baseB = A.off
            if layer == 0:
                pre_w = [wr.load(w_in_r[:, :, c0:c0 + 128], bar) for c0 in (0, 1024, 3072)]
                pre_v = wrv0.load(w_in_r[:, :, 2048:2048 + 512], bar)
            else:
                pre_w = [wr.load(w_in_r[:, :, c0:c0 + 256], bar) for c0 in (0, 2048)]
                pre_v = wrv0.load(w_in_r[:, :, 4096:4096 + 512], bar)
            g_bc = A.alloc([D], F32)
            hb = A.alloc([4, D], F32)
            tmpn = alloc_norm_tmp()
            t_g = P.dma("sync", g_bc, norm_mix_g[layer:layer + 1, :].partition_broadcast(128), "gbc", waits=bar)
            xs_free = [[], []]
            t_ls = {}

            def a_load(bq, readers):
                if 0 <= bq < TB and bq not in t_ls:
                    sl_ = bq % 4
                    t_ls[bq] = P.dma("sync", hb[:, sl_, :], src_h[bq * 128:(bq + 1) * 128, :], "hb%d" % sl_,
                                     waits=bar + (readers.get(bq - 4) or []))
            a_load(0, {})
            a_load(1, {})

            def a_iter(k, readers):
                a_load(k + 2, readers)
            norm_pipeline(TB, lambda bq: (hb[:, bq % 4, :], [t_ls[bq]] + bar), g_bc, t_g, xnT, tmpn, xs_free, on_iter=a_iter)
            if stop_after == ("A", layer):
                bar = barrier()
                final_tok = [P.dma("sync", oTd.rearrange("(kc p) t -> p kc t", p=128), xnT, "dbg", waits=bar)]
                done = True
                break

            bar = barrier()
            A.reset(baseB)
            pT = [A.alloc([512], BF16) for _ in range(4)]
            pT_free = [[] for _ in pT]
            cm = A.alloc([2, 128], BF16)
            t_cm = P.dma("gpsimd", cm, c_mask, "cm", waits=bar)
            sm = A.alloc([3, 16], F32)
            baseB2 = A.off

            if layer == 0:
                cs = A.alloc([2, T], F32)
                rot = A.alloc([128], BF16)
                gn = A.alloc([8], F32)
                t_cs = P.dma("sync", cs, c_rope, "cs", waits=bar)
                t_rot = P.dma("gpsimd", rot, c_rot, "rot", waits=bar)
                t_gn = P.dma("sync", gn, even_ret_gn, "gn", waits=bar)
                raw = A.alloc([T], BF16)
                QT2 = [A.alloc([T], BF16) for _ in range(2)]
                KT2 = [A.alloc([T], BF16) for _ in range(2)]
                V = A.alloc([TB, 4, 128], BF16)
                wrv = wrv0
                siluT2 = [A.alloc([T], BF16) for _ in range(2)]
                dec2 = [A.alloc([T], F32) for _ in range(2)]
                Abuf = A.alloc([T], BF16)
                rt = [A.alloc([512], F32) for _ in range(2)]
                otm = [A.alloc([128], F32) for _ in range(2)]
                obf = [A.alloc([128], BF16) for _ in range(2)]
                sq = A.alloc([128], F32)
                oTs2 = [A.alloc([T], BF16) for _ in range(2)]
                cols = [0, 1024, 3072]
                otm_free = [[], []]
                obf_free = [[], []]
                hist_ew = {}
                hist_ep = {}
                hist_st = {}
                pw = {}
                pst = {}
                raw_free = [[]]
                abuf_free = [[]]
                rt_free = [[], []]
                ricnt = [0]

                def load_w(h):
                    if h == 0:
                        pw[0] = pre_w
                        return
                    pw[h] = [wr.load(w_in_r[:, :, c0 + h * 128:c0 + (h + 1) * 128], bar) for c0 in cols]

                vpre = {}

                def v_proj(g):
                    k_v, w_v, t_wv = pre_v if g == 0 else vpre[g]
                    tk = proj_tm(w_v, 512, xnT, [t_wv] + bar,
                                 lambda bank, tb, waits: P.cp("vector", V[:, tb, :, :],
                                                              bank[:, 0:512].rearrange("p (g d) -> p g d", g=4), waits=waits))
                    wrv.release(k_v, [tk])
                    if g + 1 < 2:
                        vpre[g + 1] = wrv.load(w_in_r[:, :, 2048 + (g + 1) * 512:2048 + (g + 2) * 512], bar)
                    return bankF_free[0] + bankF_free[1]

                def P_gen(h):
                    siluT, QT, KT = siluT2[h % 2], QT2[h % 2], KT2[h % 2]
                    (k_q, w_q, t_wq), (k_k, w_k, t_wk), (k_g, w_g, t_wg) = pw[h]
                    t_silu = []
                    for tq in range(4):
                        b = tq % 2
                        for kc in range(KC):
                            tk = P.mm(psF[b][:, :], w_g[:, kc, 0:128], xnT[:, kc, tq * 512:(tq + 1) * 512],
                                      kc == 0, kc == KC - 1, waits=[t_wg] + bar + bankF_free[b], sig=(kc == KC - 1))
                        t_e = P.act(siluT[:, tq * 512:(tq + 1) * 512], psF[b], AF.Silu, waits=[tk, hist_ep.get(h - 2)])
                        bankF_free[b] = [t_e]
                        t_silu.append(t_e)
                        yield
                    wr.release(k_g, [tk])
                    t_qk = []
                    for qi, (wsl, t_w, kk, dst) in enumerate(((w_q, t_wq, k_q, QT), (w_k, t_wk, k_k, KT))):
                        a_toks = []
                        ev_toks = []
                        for tq in range(4):
                            b = tq % 2
                            sl = slice(tq * 512, (tq + 1) * 512)
                            for kc in range(KC):
                                tk = P.mm(psF[b][:, :], wsl[:, kc, 0:128], xnT[:, kc, sl],
                                          kc == 0, kc == KC - 1, waits=[t_w] + bar + bankF_free[b], sig=(kc == KC - 1))
                            t_e = P.act(raw[:, sl], psF[b], AF.Copy, waits=[tk] + raw_free[0])
                            bankF_free[b] = [t_e]
                            ev_toks.append(t_e)
                            a_toks.append(P.tt("gpsimd", Abuf[:, sl], raw[:, sl], cs[:, 0, sl], ALU.mult,
                                               waits=[t_e, t_cs] + abuf_free[0]))
                            yield
                        wr.release(kk, [tk])
                        tcs = []
                        for tq in range(4):
                            b = tq % 2
                            sl = slice(tq * 512, (tq + 1) * 512)
                            tk = P.mm(psF[b][:, :], rot, raw[:, sl], True, True,
                                      waits=ev_toks + [t_rot] + bankF_free[b], sig=True)
                            r = ricnt[0] % 2
                            tb_ = P.tt("vector", rt[r], psF[b][:, :], cs[:, 1, sl], ALU.mult, waits=[tk, t_cs] + rt_free[r])
                            bankF_free[b] = [tb_]
                            eng = "gpsimd" if ricnt[0] % 2 == 0 else "vector"
                            tc = P.tt(eng, dst[:, sl], rt[r], Abuf[:, sl], ALU.add, waits=[tb_] + a_toks)
                            rt_free[r] = [tc]
                            tcs.append(tc)
                            ricnt[0] += 1
                            yield
                        raw_free[0] = [tk, a_toks[-1]]
                        abuf_free[0] = tcs[-2:]
                        t_qk += tcs
                    if h + 1 < 8:
                        load_w(h + 1)
                    pst[h] = {"t_in": t_qk, "t_silu": t_silu}

                load_w(0)
                t_v = v_proj(0)
                for _ in P_gen(0):
                    pass
                t_decs = {0: P.dma("sync", dec2[0], c_dec[0], "dec0", waits=bar)}
                for h in range(8):
                    siluT, QT, KT = siluT2[h % 2], QT2[h % 2], KT2[h % 2]
                    dec = dec2[h % 2]
                    oTs = oTs2[h % 2]
                    t_dec = t_decs[h]
                    if h + 1 < 8:
                        t_decs[h + 1] = P.dma("sync", dec2[(h + 1) % 2], c_dec[h + 1], "dec%d" % ((h + 1) % 2),
                                              waits=bar + [hist_ew.get(h - 1)])
                    t_in = pst[h]["t_in"] + t_v
                    t_silu = pst[h]["t_silu"]

                    def elementwise(m, i, jtop, n, ps, pt, waits, dec=dec, t_dec=t_dec):
                        o = (i - jtop) * 128
                        return P.tt("vector", pt[:, 0:n * 128], ps[:, 0:n * 128], dec[:, o:o + n * 128], ALU.mult,
                                    waits=waits + [t_dec])

                    ep_last = [None]

                    def ep1(i, lasts):
                        s = i % 2
                        acc = psF[2 + 2 * (i % 2)][:, 0:128]
                        t_sq = P.act(sq, acc, AF.Square, waits=lasts, accum_out=sm[:, 0, i:i + 1])
                        t_v_ = P.ts("gpsimd", sm[:, 1, i:i + 1], sm[:, 0, i:i + 1], 1.0 / 128, GN_EPS, ALU.mult, ALU.add, waits=[t_sq])
                        t_r = P.tt("gpsimd", sm[:, 2, i:i + 1], sm[:, 1, i:i + 1], neghalf[:, 0:1], ALU.pow, waits=[t_v_, t_nh])
                        t_o = P.act(obf[s], acc, AF.Copy, waits=[t_r] + lasts + obf_free[s], scale=sm[:, 2, i:i + 1])
                        return [t_sq, t_o], t_o

                    def ep2(i, t_o, h=h, oTs=oTs, siluT=siluT, t_silu=t_silu, ep_last=ep_last):
                        s = i % 2
                        tk = P.tr(psT[s][:, 0:128], obf[s], ident, waits=[t_o] + bankT_free[s], sig=True)
                        obf_free[s] = [tk]
                        sl = slice(i * 128, (i + 1) * 128)
                        t_e = P.stt(oTs[:, sl], psT[s][:, 0:128], gn[:, h:h + 1], siluT[:, sl],
                                    ALU.mult, ALU.mult, waits=[tk, t_gn] + t_silu + [hist_st.get(h - 2)])
                        bankT_free[s] = [t_e]
                        ep_last[0] = t_e

                    gen = P_gen(h + 1) if h + 1 < 8 else None
                    hist_ew[h] = attention(1, [QT], [KT], lambda j, hs=h % 4: V[:, j, hs, :], 128, t_in, elementwise, (ep1, ep2),
                                           pT, pT_free, filler=gen, sbanks=[3, 5])
                    if gen is not None:
                        for _ in gen:
                            pass
                    hist_ep[h] = ep_last[0]
                    hist_st[h] = P.dma("sync", oTd[h * 128:(h + 1) * 128, :], oTs, "ot%d" % (h % 2), waits=[ep_last[0]])
                    if (h + 1) % 4 == 0 and h + 1 < 8:
                        t_v = v_proj((h + 1) // 4)
                if stop_after == ("Bret", layer):
                    bar = barrier()
                    final_tok = [bar]
                    done = True
                    break

                bar = barrier()
                A.reset(baseB2)
                wff = A.alloc([KC, 8], BF16)
                bfc = A.alloc([1], F32)
                sel = A.alloc([8, 128], F32)
                id8 = A.alloc([8], F32)
                ones = A.alloc([T], F32)
                lf = A.alloc([T], F32)
                Fc = A.alloc([T], F32)
                Fs = A.alloc([TB, 8], F32)
                Fb = A.alloc([8, TB], F32)
                bt = A.alloc([8, TB, TB], F32)
                QT2 = [A.alloc([T], BF16) for _ in range(2)]
                KT2 = [A.alloc([T], BF16) for _ in range(2)]
                V = A.alloc([TB, 4, 130], BF16)
                wrv = WRing(1, [KC, 512], "wrv")
                vnxt = None
                rl = A.alloc([16], F32)
                obf = [A.alloc([128], BF16) for _ in range(2)]
                oTs2 = [A.alloc([T], BF16) for _ in range(2)]
                obf_free = [[], []]
                hist_st = {}
                t_wff = P.dma("gpsimd", wff, w_in_r[:, :, 7168:7176], "wff", waits=bar)
                t_bf = P.dma("sync", bfc[0:8, :], even_b_f, "bf", waits=bar)
                t_sel = P.dma("sync", sel[0:8, :, :], c_sel, "sel", waits=bar)
                t_i8 = P.dma("sync", id8[0:8, :], c_id8, "id8", waits=bar)
                t_on = P.memset("gpsimd", ones[0:8, :], 1.0, waits=bar)
                t_v1 = P.memset("gpsimd", V[:, :, :, 128:130], 1.0, waits=bar)
                for tq in range(4):
                    b = tq % 2
                    for kc in range(KC):
                        tk = P.mm(psF[b][0:8, :], wff[:, kc, :], xnT[:, kc, tq * 512:(tq + 1) * 512],
                                  kc == 0, kc == KC - 1, waits=[t_wff] + bar + bankF_free[b], sig=(kc == KC - 1))
                    t_e = P.act(lf[0:8, tq * 512:(tq + 1) * 512], psF[b][0:8, :], AF.Sigmoid, waits=[tk, t_bf],
                                bias=bfc[0:8, 0:1])
                    bankF_free[b] = [t_e]
                t_ln = P.act(lf[0:8, :], lf[0:8, :], AF.Ln, waits=bankF_free[0] + bankF_free[1])
                t_F = P.op("vector", lambda e: e.tensor_tensor_scan(out=Fc[0:8, :], data0=ones[0:8, :], data1=lf[0:8, :],
                                                                    initial=0.0, op0=ALU.mult, op1=ALU.add),
                           waits=[t_ln, t_on])
                for j in range(TB):
                    tk = P.op("tensor", lambda e, j=j: e.transpose(out=psF[2][:, j * 8:(j + 1) * 8],
                                                                   in_=Fc[0:8, j * 128:(j + 1) * 128], identity=id8[0:8, :]),
                              waits=[t_F, t_i8] + bar, sig=(j == TB - 1))
                t_fs = P.cp("vector", Fs, psF[2][:, 0:128].rearrange("p (j h) -> p j h", h=8), waits=[tk])
                for h in range(8):
                    tk = P.mm(psF[3][:, h * 16:(h + 1) * 16], sel[0:8, h, :],
                              Fc[0:8, :].rearrange("p (i c) -> p i c", c=128)[:, :, 0], True, True,
                              waits=[t_F, t_sel] + bar, sig=(h == 7))
                t_fb = P.cp("vector", Fb, psF[3][:, 0:128].rearrange("p (h i) -> p h i", i=16), waits=[tk])
                t_bt = []
                for j in range(TB):
                    t_bt.append(P.tt("gpsimd" if j % 2 == 0 else "vector", bt[:, :, j, :], Fb,
                                     Fs[:, j, :].unsqueeze(2).to_broadcast([128, 8, TB]), ALU.subtract, waits=[t_fs, t_fb]))
                t_bt = t_bt[-2:]
                nxt = None
                colsf = [4096, 5120]
                pw = {}
                pst = {}

                def load_w(h):
                    pw[h] = [wr.load(w_in_r[:, :, c0 + h * 128:c0 + (h + 1) * 128], bar) for c0 in colsf]

                vpre = {}

                def v_proj(g):
                    k_v, w_v, t_wv = vpre[g] if g in vpre else wrv.load(w_in_r[:, :, 6144 + g * 512:6144 + (g + 1) * 512], bar)
                    tk = proj_tm(w_v, 512, xnT, [t_wv] + bar,
                                 lambda bank, tb, waits: P.cp("vector", V[:, tb, :, 0:128],
                                                              bank[:, 0:512].rearrange("p (g d) -> p g d", g=4), waits=waits))
                    wrv.release(k_v, [tk])
                    if g + 1 < 2:
                        vpre[g + 1] = wrv.load(w_in_r[:, :, 6144 + (g + 1) * 512:6144 + (g + 2) * 512], bar)
                    return bankF_free[0] + bankF_free[1]

                def P_gen(h):
                    (k_q, w_q, t_wq), (k_k, w_k, t_wk) = pw[h]
                    toks = []
                    for dst, wsl, t_w, kk in ((QT2[h % 2], w_q, t_wq, k_q), (KT2[h % 2], w_k, t_wk, k_k)):
                        for tq in range(4):
                            b = tq % 2
                            sl = slice(tq * 512, (tq + 1) * 512)
                            for kc in range(KC):
                                tk = P.mm(psF[b][:, :], wsl[:, kc, 0:128], xnT[:, kc, sl],
                                          kc == 0, kc == KC - 1, waits=[t_w] + bar + bankF_free[b], sig=(kc == KC - 1))
                            t_e = P.cp("vector", dst[:, sl], psF[b], waits=[tk])
                            bankF_free[b] = [t_e]
                            toks.append(t_e)
                            yield
                        wr.release(kk, [tk])
                    if h + 1 < 8:
                        load_w(h + 1)
                    pst[h] = toks[-1:]

                load_w(0)
                t_v = v_proj(0)
                for _ in P_gen(0):
                    pass
                for h in range(8):
                    oTs = oTs2[h % 2]
                    QT, KT = QT2[h % 2], KT2[h % 2]
                    t_in = pst[h] + t_v + [t_bt, t_v1, t_cm]

                    def elementwise(m, i, jtop, n, ps, pt, waits, h=h):
                        toks = []
                        for c in range(n):
                            jj = jtop - c
                            t_e = P.act(pt[:, c * 128:(c + 1) * 128], ps[:, c * 128:(c + 1) * 128], AF.Exp,
                                        waits=waits, scale=SCALE, bias=bt[:, h, jj, i:i + 1])
                            if jj == i:
                                toks.append(P.tt("vector", pt[:, c * 128:(c + 1) * 128], pt[:, c * 128:(c + 1) * 128],
                                                 cm[:, 0, :], ALU.mult, waits=[t_e]))
                        toks.append(t_e)
                        return toks

                    ep_last = [None]

                    def ep1(i, lasts):
                        s = i % 2
                        bidx = 2 + 2 * (i % 2)
                        t_r = P.recip(rl[:, i:i + 1], psF[bidx][:, 128:129], waits=lasts)
                        t_o = P.ts("vector", obf[s], psF[bidx][:, 0:128], rl[:, i:i + 1], None, ALU.mult,
                                   waits=[t_r] + lasts + obf_free[s])
                        return [t_o], t_o

                    def ep2(i, t_o, h=h, oTs=oTs, ep_last=ep_last):
                        s = i % 2
                        tk = P.tr(psT[s][:, 0:128], obf[s], ident, waits=[t_o] + bankT_free[s], sig=True)
                        obf_free[s] = [tk]
                        t_e = P.cp("vector", oTs[:, i * 128:(i + 1) * 128], psT[s][:, 0:128], waits=[tk, hist_st.get(h - 2)])
                        bankT_free[s] = [t_e]
                        ep_last[0] = t_e

                    gen = P_gen(h + 1) if h + 1 < 8 else None
                    attention(1, [QT], [KT], lambda j, hs=h % 4: V[:, j, hs, 0:129], 129, t_in, elementwise, (ep1, ep2), pT, pT_free,
                              filler=gen, sbanks=[3, 5])
                    if gen is not None:
                        for _ in gen:
                            pass
                    hist_st[h] = P.dma("sync", oTd[(8 + h) * 128:(9 + h) * 128, :], oTs, "ot%d" % (h % 2), waits=[ep_last[0]])
                    if (h + 1) % 4 == 0 and h + 1 < 8:
                        t_v = v_proj((h + 1) // 4)
            else:
                lamv = A.alloc([4, 128], F32)
                lsm = A.alloc([8], F32)
                gsub = A.alloc([256], F32)
                QT2 = [[A.alloc([T], BF16) for _ in range(2)] for _ in range(2)]
                KT2 = [[A.alloc([T], BF16) for _ in range(2)] for _ in range(2)]
                pbank = psT[1].bitcast(F32)
                V = A.alloc([TB, 2, 258], BF16)
                wrv = wrv0
                vnxt = None
                rl = A.alloc([3, 16], F32)
                o1 = [A.alloc([256], F32) for _ in range(2)]
                o2 = [A.alloc([256], F32) for _ in range(2)]
                sq = A.alloc([256], F32)
                obf = [A.alloc([256], BF16) for _ in range(2)]
                oTs2 = [A.alloc([2, T], BF16) for _ in range(2)]
                buf_free = [[], []]
                hist_st = {}
                t_lam = P.dma("sync", lamv.rearrange("p a b -> p (a b)"),
                              odd_lam.rearrange("a b -> (a b)").partition_broadcast(128), "lam", waits=bar)
                t_gs = P.dma("sync", gsub, odd_subln_g.partition_broadcast(128), "gs", waits=bar)
                t_v1 = P.memset("gpsimd", V[:, :, :, 256:258], 1.0, waits=bar)
                t_a = P.stt(sq[:, 0:128], lamv[:, 0, :], 1.0, lamv[:, 1, :], ALU.mult, ALU.mult, waits=[t_lam], accum_out=lsm[:, 0:1])
                t_b = P.stt(sq[:, 128:256], lamv[:, 2, :], 1.0, lamv[:, 3, :], ALU.mult, ALU.mult, waits=[t_lam], accum_out=lsm[:, 1:2])
                t_e1 = P.act(lsm[:, 2:4], lsm[:, 0:2], AF.Exp, waits=[t_a, t_b])
                t_d = P.tt("gpsimd", lsm[:, 4:5], lsm[:, 3:4], lsm[:, 2:3], ALU.subtract, waits=[t_e1])
                t_nl = P.ts("vector", lsm[:, 5:6], lsm[:, 4:5], -LAMBDA_INIT, None, ALU.add, waits=[t_d])
                t_g2 = P.ts("gpsimd", gsub, gsub, 1.0 - LAMBDA_INIT, None, ALU.mult, waits=[t_gs])
                colsd = [0, 2048]
                pw = {}
                pst = {}
                pfree = [[]]

                def load_w(h):
                    if h == 0:
                        pw[0] = pre_w
                        return
                    pw[h] = [wr.load(w_in_r[:, :, c0 + h * 256:c0 + (h + 1) * 256], bar) for c0 in colsd]

                vpre = {}

                def v_proj(g):
                    k_v, w_v, t_wv = pre_v if g == 0 else vpre[g]
                    tk = proj_tm(w_v, 512, xnT, [t_wv] + bar,
                                 lambda bank, tb, waits: P.cp("vector", V[:, tb, :, 0:256],
                                                              bank[:, 0:512].rearrange("p (g d) -> p g d", g=2), waits=waits))
                    wrv.release(k_v, [tk])
                    if g + 1 < 4:
                        vpre[g + 1] = wrv.load(w_in_r[:, :, 4096 + (g + 1) * 512:4096 + (g + 2) * 512], bar)
                    return bankF_free[0] + bankF_free[1]

                def P_gen(h):
                    (k_q, w_q, t_wq), (k_k, w_k, t_wk) = pw[h]
                    toks = []
                    for dsts, wsl, t_w, kk in ((QT2[h % 2], w_q, t_wq, k_q), (KT2[h % 2], w_k, t_wk, k_k)):
                        for m in range(2):
                            for tq in range(4):
                                sl = slice(tq * 512, (tq + 1) * 512)
                                for kc in range(KC):
                                    tk = P.mm(pbank, wsl[:, kc, m * 128:(m + 1) * 128], xnT[:, kc, sl],
                                              kc == 0, kc == KC - 1, waits=[t_w] + bar + pfree[0], sig=(kc == KC - 1))
                                t_e = P.act(dsts[m][:, sl], pbank, AF.Copy, waits=[tk])
                                pfree[0] = [t_e]
                                toks.append(t_e)
                                yield
                        wr.release(kk, [tk])
                    if h + 1 < 8:
                        load_w(h + 1)
                    pst[h] = toks[-1:]

                load_w(0)
                t_v = v_proj(0)
                for _ in P_gen(0):
                    pass
                for h in range(8):
                    oTs = oTs2[h % 2]
                    QT, KT = QT2[h % 2], KT2[h % 2]
                    t_in = pst[h] + t_v + [t_v1, t_cm, t_nl, t_g2]

                    def elementwise(m, i, jtop, n, ps, pt, waits):
                        t_e = P.act(pt[:, 0:n * 128], ps[:, 0:n * 128], AF.Exp, waits=waits, scale=SCALE)
                        if jtop == i:
                            return [t_e, P.tt("vector", pt[:, 0:128], pt[:, 0:128], cm[:, 1, :], ALU.mult, waits=[t_e])]
                        return [t_e]

                    ep_last = [None]

                    def ep1(i, lasts):
                        s = i % 2
                        b1 = 2 + 2 * (i % 2)
                        b2 = b1 + 1
                        t_r1 = P.recip(rl[:, 0, i:i + 1], psF[b1][:, 256:257], waits=lasts)
                        t_r2 = P.recip(rl[:, 1, i:i + 1], psF[b2][:, 256:257], waits=lasts)
                        t_r3 = P.ts("vector", rl[:, 2, i:i + 1], rl[:, 1, i:i + 1], lsm[:, 5:6], None, ALU.mult, waits=[t_r2, t_nl])
                        t_1 = P.ts("vector", o1[s], psF[b1][:, 0:256], rl[:, 0, i:i + 1], None, ALU.mult,
                                   waits=[t_r1] + lasts + buf_free[s])
                        t_2 = P.stt(o2[s], psF[b2][:, 0:256], rl[:, 2, i:i + 1], o1[s], ALU.mult, ALU.add, waits=[t_1, t_r3] + lasts)
                        t_3 = P.stt(sq, o2[s], 1.0, o2[s], ALU.mult, ALU.mult, waits=[t_2], accum_out=sm[:, 0, i:i + 1])
                        t_r = rstd_of(sm[:, 0, i:i + 1], sm[:, 1, i:i + 1], sm[:, 2, i:i + 1], 256, GN_EPS, [t_3])
                        t_o = P.stt(obf[s], o2[s], sm[:, 2, i:i + 1], gsub, ALU.mult, ALU.mult, waits=[t_r, t_g2])
                        return [t_1, t_2], t_o

                    def ep2(i, t_o, h=h, oTs=oTs, ep_last=ep_last):
                        s = i % 2
                        for c in range(2):
                            tk = P.tr(psT[0][:, s * 256 + c * 128:s * 256 + (c + 1) * 128], obf[s][:, c * 128:(c + 1) * 128], ident,
                                      waits=[t_o] + bankT_free[s], sig=(c == 1))
                        buf_free[s] = [tk]
                        t_e = P.cp("vector", oTs[:, :, i * 128:(i + 1) * 128],
                                   psT[0][:, s * 256:(s + 1) * 256].rearrange("p (c t) -> p c t", c=2), waits=[tk, hist_st.get(h - 2)])
                        bankT_free[s] = [t_e]
                        ep_last[0] = t_e

                    gen = P_gen(h + 1) if h + 1 < 8 else None
                    attention(2, QT, KT, lambda j, hs=h % 2: V[:, j, hs, 0:257], 257, t_in, elementwise, (ep1, ep2), pT, pT_free,
                              filler=gen)
                    if gen is not None:
                        for _ in gen:
                            pass
                    hist_st[h] = P.dma("sync", oTd[h * 256:(h + 1) * 256, :].rearrange("(c p) t -> p c t", p=128), oTs,
                                       "ot%d" % (h % 2), waits=[ep_last[0]])
                    if (h + 1) % 2 == 0 and h + 1 < 8:
                        t_v = v_proj((h + 1) // 2)
            if stop_after == ("B", layer):
                bar = barrier()
                final_tok = [bar]
                done = True
                break

            bar = barrier()
            A.reset()
            oT = A.alloc([KC, T], BF16)
            wo = WRing(2, [KC, 512], "wo")
            hs = [A.alloc([TB, 512], F32) for _ in range(2)]
            hs_free = [[], []]
            oTd_r = oTd.rearrange("(kc p) t -> p kc t", p=128)
            t_otq = [P.dma("sync", oT[:, :, q * 512:(q + 1) * 512], oTd_r[:, :, q * 512:(q + 1) * 512], "oT%d" % q, waits=bar)
                     for q in range(4)]
            w_out_r = w_out.rearrange("(kc p) n -> p kc n", p=128)
            bi = 0
            def c1_load(nb):
                s = nb % 2
                return (wo.load(w_out_r[:, :, nb * 512:(nb + 1) * 512], bar),
                        P.dma("sync", hs[s], src_h[:, nb * 512:(nb + 1) * 512].rearrange("(tb p) n -> p tb n", p=128),
                              "hs%d" % s, waits=bar + hs_free[s]))
            c1n = c1_load(0)
            for nb in range(4):
                s = nb % 2
                (k_w, wsl, t_w), t_h = c1n
                if nb + 1 < 4:
                    c1n = c1_load(nb + 1)
                adds = []
                for tb in range(TB):
                    b = bi % 4
                    bi += 1
                    for kc in range(KC):
                        tk = P.mm(psF[b][:, :], oT[:, kc, tb * 128:(tb + 1) * 128], wsl[:, kc, :],
                                  kc == 0, kc == KC - 1, waits=[t_otq[tb // 4], t_w] + bar + bankF_free[b], sig=(kc == KC - 1))
                    t_a = P.tt("vector", hs[s][:, tb, :], hs[s][:, tb, :], psF[b][:, :], ALU.add, waits=[tk, t_h])
                    bankF_free[b] = [t_a]
                    adds.append(t_a)
                wo.release(k_w, [tk])
                t_st = P.dma("sync", mid_h[:, nb * 512:(nb + 1) * 512].rearrange("(tb p) n -> p tb n", p=128), hs[s],
                             "hso%d" % s, waits=[adds[-1]])
                hs_free[s] = [t_st]
            if stop_after == ("C1", layer):
                bar = barrier()
                final_tok = [bar]
                done = True
                break

            last_layer = (layer == nlayers - 1)
            bar = barrier()
            A.reset()
            hacc = A.alloc([8, D], F32)
            xn2T = A.alloc([KC, 1024], BF16)
            g_bc = A.alloc([D], F32)
            w1r = WRing(2, [KC, 512], "w1r")
            w2r = WRing(2, [4, D], "w2r")
            aT = [A.alloc([4, 1024], BF16) for _ in range(2)]
            rtmp = [A.alloc([512], F32) for _ in range(2)]
            gf = A.alloc([D], F32)
            xs_view = gf.bitcast(BF16).rearrange("p (a b) -> p a b", a=2)
            tmpn = {"junk": aT[0][:, 0:2, :].rearrange("p a b -> p (a b)"), "xs": xs_view,
                    "ss": A.alloc([TB], F32), "v": A.alloc([TB], F32), "rs": A.alloc([TB], F32)}
            t_g = P.dma("sync", g_bc, norm_mlp_g[layer:layer + 1, :].partition_broadcast(128), "gbc", waits=bar)
            w1_r = mlp_w1[layer].rearrange("(kc p) f -> p kc f", p=128)
            w2_r = mlp_w2[layer].rearrange("(fc p) n -> p fc n", p=128)
            NFG = DFF // 512

            w1l = {}
            w2l = {}

            def issue_w1(idx):
                if idx < 2 * NFG:
                    fg_ = idx % NFG
                    w1l[idx] = w1r.load(w1_r[:, :, fg_ * 512:(fg_ + 1) * 512], bar)

            def issue_w2(idx):
                if idx < 2 * NFG:
                    fg_ = idx % NFG
                    w2l[idx] = w2r.load(w2_r[:, fg_ * 4:(fg_ + 1) * 4, :], bar)
            issue_w1(0)
            issue_w2(0)
            issue_w1(1)
            issue_w2(1)
            st_tok = {}
            xs_extra = []
            aT_free = [[], []]
            rtmp_free = [[], []]
            cnt = {"ri": 0, "yb": 0}
            hst = {}

            def H(idx):
                fg = idx % NFG
                a = fg % 2
                k1, w1s, t_w1 = w1l[idx]
                sq_toks = []
                for fc in range(4):
                    for th in range(2):
                        b = (fc * 2 + th) % 2
                        for kc in range(KC):
                            tk = P.mm(psF[b][:, :], w1s[:, kc, fc * 128:(fc + 1) * 128],
                                      xn2T[:, kc, th * 512:(th + 1) * 512], kc == 0, kc == KC - 1,
                                      waits=[t_w1] + hst["xn_all"] + bar + bankF_free[b], sig=(kc == KC - 1))
                        r = cnt["ri"] % 2
                        cnt["ri"] += 1
                        t_r = P.act(rtmp[r], psF[b][:, :], AF.Relu, waits=[tk] + rtmp_free[r])
                        bankF_free[b] = [t_r]
                        t_s = P.tt("gpsimd", aT[a][:, fc, th * 512:(th + 1) * 512], rtmp[r], rtmp[r], ALU.mult,
                                   waits=[t_r] + aT_free[a])
                        rtmp_free[r] = [t_s]
                        sq_toks.append(t_s)
                w1r.release(k1, [tk])
                issue_w1(idx + 2)
                hst[("sq", idx)] = sq_toks[-1]

            def Y(idx):
                half, fg = idx // NFG, idx % NFG
                a = fg % 2
                k2, w2s, t_w2 = w2l[idx]
                t_sq = hst[("sq", idx)]
                xn_all = hst["xn_all"]
                hacc_tok = hst["hacc_tok"]
                pend_fn = None
                for tb in range(8):
                    for nb in range(4):
                        b = 2 + (cnt["yb"] % 4)
                        cnt["yb"] += 1
                        for fc in range(4):
                            tk = P.mm(psF[b][:, :], aT[a][:, fc, tb * 128:(tb + 1) * 128],
                                      w2s[:, fc, nb * 512:(nb + 1) * 512], fc == 0, fc == 3,
                                      waits=[t_w2, t_sq] + bar + bankF_free[b], sig=(fc == 3))
                        t_a = P.tt("vector", hacc[:, tb, nb * 512:(nb + 1) * 512], hacc[:, tb, nb * 512:(nb + 1) * 512],
                                   psF[b][:, :], ALU.add, waits=[tk] + hacc_tok[tb] + xn_all)
                        bankF_free[b] = [t_a]
                    if fg == NFG - 1:
                        r0 = half * 1024 + tb * 128
                        if not last_layer:
                            st_tok[tb] = P.dma("sync", dst_h[r0:r0 + 128, :], hacc[:, tb, :], "hst%d" % tb, waits=[t_a])
                        else:
                            c = 8 + tb
                            t_ss = P.act(tmpn["junk"], hacc[:, tb, :], AF.Square, waits=[t_a], accum_out=tmpn["ss"][:, c:c + 1])
                            t_rr = rstd_of(tmpn["ss"][:, c:c + 1], tmpn["v"][:, c:c + 1], tmpn["rs"][:, c:c + 1], D, RMS_EPS, [t_ss])
                            if pend_fn is not None:
                                ptb, pc, pt_r, pr0 = pend_fn
                                t_x = P.stt(hacc[:, ptb, :], hacc[:, ptb, :], tmpn["rs"][:, pc:pc + 1], gf, ALU.mult, ALU.mult,
                                            waits=[pt_r, hst["t_gf"]])
                                st_tok[ptb] = P.dma("sync", y[pr0:pr0 + 128, :], hacc[:, ptb, :], "yo%d" % ptb, waits=[t_x])
                            pend_fn = (tb, c, t_rr, r0)
                if pend_fn is not None:
                    ptb, pc, pt_r, pr0 = pend_fn
                    t_x = P.stt(hacc[:, ptb, :], hacc[:, ptb, :], tmpn["rs"][:, pc:pc + 1], gf, ALU.mult, ALU.mult,
                                waits=[pt_r, hst["t_gf"]])
                    st_tok[ptb] = P.dma("sync", y[pr0:pr0 + 128, :], hacc[:, ptb, :], "yo%d" % ptb, waits=[t_x])
                    hst["xs_extra"] = [t_x]
                hst["hacc_tok"] = [[t_a] for _ in range(8)]
                aT_free[a] = [tk]
                w2r.release(k2, [tk])
                issue_w2(idx + 2)

            hst["xs_extra"] = []
            for half in range(2):
                t_hl = []
                for tb in range(8):
                    r0 = half * 1024 + tb * 128
                    t_hl.append(P.dma("sync", hacc[:, tb, :], mid_h[r0:r0 + 128, :], "hacc%d" % tb,
                                      waits=bar + [st_tok.get(tb)]))
                xs_free = [list(hst["xs_extra"]), list(hst["xs_extra"])]
                norm_pipeline(8, lambda bq: (hacc[:, bq, :], [t_hl[bq]] + bar), g_bc, t_g, xn2T, tmpn, xs_free)
                hst["xn_all"] = bankT_free[0] + bankT_free[1]
                if last_layer:
                    hst["t_gf"] = P.dma("sync", gf, final_g.partition_broadcast(128), "gf", waits=xs_free[0] + xs_free[1])
                hst["hacc_tok"] = [[t] for t in t_hl]
                H(half * NFG)
                for fg in range(NFG):
                    if fg + 1 < NFG:
                        H(half * NFG + fg + 1)
                    Y(half * NFG + fg)
        if not done:
            bar = barrier()
            final_tok = list(bar)
        P.emit(P._flat(final_tok, []))
    return nc


_NC_CACHE = {}


def _consts():
    half = 64
    inv = (10000.0 ** (-np.arange(half, dtype=np.float32) / half)).astype(np.float32)
    pos = np.arange(T, dtype=np.float32)
    ang = (pos[:, None] * inv[None, :]).astype(np.float32)
    cos = np.cos(ang).astype(np.float32).T
    sin = np.sin(ang).astype(np.float32).T
    rope = np.zeros((128, 2, T), np.float32)
    rope[0:64, 0], rope[64:128, 0] = cos, cos
    rope[0:64, 1], rope[64:128, 1] = sin, sin
    rot = np.zeros((128, 128), np.float32)
    for dp in range(64):
        rot[dp + 64, dp] = -1.0
        rot[dp, dp + 64] = 1.0
    dec = np.zeros((8, 128, T), np.float32)
    sl = np.arange(128, dtype=np.float64)[:, None]
    tt = np.arange(T, dtype=np.float64)[None, :]
    for h in range(8):
        log_g = np.log1p(-(2.0 ** (-5.0 - h)))
        m = np.exp(log_g * (tt - sl))
        d = np.exp(log_g * np.abs(tt[:, :128] - sl))
        cs_, ct_ = (sl // 64), (tt[:, :128] // 64)
        diag = np.where(cs_ == ct_, d, np.where(cs_ < ct_, m[:, :128], 0.0))
        m[:, :128] = diag
        dec[h] = (m * (HD ** -0.5)).astype(np.float32)
    mask = np.zeros((128, 2, 128), np.float32)
    s_ = np.arange(128)[:, None]
    t_ = np.arange(128)[None, :]
    mask[:, 0, :] = (s_ <= t_)
    mask[:, 1, :] = ~((s_ >= 64) & (t_ < 64))
    sel = np.zeros((8, 8, 128), np.float32)
    for h in range(8):
        sel[h, h, :] = 1.0
    return {"c_rope": rope, "c_rot": rot, "c_dec": dec, "c_mask": mask, "c_sel": sel,
            "c_id8": np.eye(8, dtype=np.float32)}


def make_in_maps(inputs):
    f = lambda a: np.ascontiguousarray(np.asarray(a, dtype=np.float32))
    shared = {
        "norm_mix_g": f(inputs["norm_mix_g"]),
        "norm_mlp_g": f(inputs["norm_mlp_g"]),
        "even_w_in": f(inputs["even_w_in"])[0],
        "even_b_f": f(inputs["even_b_f"]).reshape(8, 1),
        "even_ret_gn": np.ascontiguousarray(f(inputs["even_ret_gn"]).reshape(8, 128).T),
        "even_w_out": f(inputs["even_w_out"])[0],
        "odd_w_in": f(inputs["odd_w_in"])[0],
        "odd_lam": np.ascontiguousarray(np.concatenate([f(inputs["odd_lambda_q1"]), f(inputs["odd_lambda_k1"]),
                                                        f(inputs["odd_lambda_q2"]), f(inputs["odd_lambda_k2"])], axis=0)),
        "odd_subln_g": f(inputs["odd_subln_g"]).reshape(1, 256),
        "odd_w_out": f(inputs["odd_w_out"])[0],
        "mlp_w1": f(inputs["mlp_w1"]),
        "mlp_w2": f(inputs["mlp_w2"]),
        "final_g": f(inputs["final_g"]).reshape(1, D),
    }
    shared.update(_consts())
    xs = f(inputs["x"])
    return [dict(shared, x=xs[b]) for b in range(xs.shape[0])]


def kernel(**inputs):
    if "nc" not in _NC_CACHE:
        _NC_CACHE["nc"] = build_program()
    nc = _NC_CACHE["nc"]
    in_maps = make_in_maps(inputs)
    res = run_bass_kernel_spmd(nc, in_maps, core_ids=list(range(len(in_maps))))
    return np.stack([np.asarray(r["y"], dtype=np.float32) for r in res.results], axis=0)
```
